# Optimizing a Trainium2 kernel written in Bass

```python
import math
import jax, jax.numpy as jnp
from jax import lax
import numpy as np

D_MODEL = 1024
BATCH = 2
SEQ = 8192
DEPTH = 2
DEC_BATCH = 16
DEC_SEQ = 4096
PAST_LEN = 128

N_EVEN = (DEPTH + 1) // 2
N_ODD = DEPTH // 2
BLOCK = 128
ROPE_THETA = 10000.0
EPS = 1e-6
HEAD_DIM = 64

CONV_CH = D_MODEL // 2
CONV_WIDTH = 31
SWA_HEADS = (D_MODEL // 2) // HEAD_DIM
SWA_KV_HEADS = 2
WINDOW = 128
DIFF_HEADS = (D_MODEL // 2) // (2 * HEAD_DIM)
DIFF_WIDTH = DIFF_HEADS * 2 * HEAD_DIM
SSM_INNER = D_MODEL // 2
SSM_HEAD_DIM = 64
SSM_HEADS = SSM_INNER // SSM_HEAD_DIM
SSM_GROUPS = 2
SSM_STATE = 128
SSM_CONV = 5
SSM_CHUNK = 128
SSM_CONV_DIM = SSM_INNER + 2 * SSM_GROUPS * SSM_STATE

EVEN_SPLITS = (2 * CONV_CH, CONV_CH, SWA_HEADS * HEAD_DIM, SWA_KV_HEADS * HEAD_DIM,
               SWA_KV_HEADS * HEAD_DIM, SWA_HEADS * HEAD_DIM)
ODD_SPLITS = (DIFF_WIDTH, DIFF_WIDTH, DIFF_WIDTH, DIFF_WIDTH,
              SSM_INNER, SSM_CONV_DIM, SSM_HEADS, SSM_HEADS)
IN_EVEN = sum(EVEN_SPLITS)
IN_ODD = sum(ODD_SPLITS)
OUT_EVEN = CONV_CH + SWA_HEADS * HEAD_DIM
OUT_ODD = DIFF_WIDTH + SSM_INNER

kernel_name = "hybrid_conv_swa_diffattn_ssd_encoder"

F32 = jnp.float32


def split_cols(u, sizes):
    idx = [int(c) for c in np.cumsum(sizes)[:-1]]
    return jnp.split(u, idx, axis=-1)


def rmsnorm(x, g):
    xf = x.astype(F32)
    y = xf * lax.rsqrt(jnp.mean(xf * xf, axis=-1, keepdims=True) + EPS)
    return (y * g.astype(F32)).astype(x.dtype)


def layernorm(x, g, b):
    xf = x.astype(F32)
    mu = jnp.mean(xf, axis=-1, keepdims=True)
    var = jnp.mean(jnp.square(xf - mu), axis=-1, keepdims=True)
    return ((xf - mu) * lax.rsqrt(var + EPS) * g.astype(F32) + b.astype(F32)).astype(x.dtype)


def rope_tables(seq):
    inv = 1.0 / (ROPE_THETA ** (jnp.arange(0, HEAD_DIM, 2, dtype=F32) / HEAD_DIM))
    pos = jnp.arange(seq, dtype=F32)
    f = pos[:, None] * inv[None, :]
    emb = jnp.concatenate([f, f], axis=-1)
    return jnp.cos(emb)[:, None, :], jnp.sin(emb)[:, None, :]


def apply_rope(x, cos, sin):
    xf = x.astype(F32)
    x1, x2 = jnp.split(xf, 2, axis=-1)
    rot = jnp.concatenate([-x2, x1], axis=-1)
    return (xf * cos + rot * sin).astype(x.dtype)


def depthwise_conv(x, w, b):
    width, ch = w.shape
    out = lax.conv_general_dilated(
        x, w[:, None, :].astype(x.dtype), window_strides=(1,),
        padding=[((width - 1) // 2, width // 2)],
        dimension_numbers=('NWC', 'WIO', 'NWC'), feature_group_count=ch)
    return out + b.astype(x.dtype)


def windowed_gqa(q, k, v, sink):
    b, s, h, dh = q.shape
    hkv = k.shape[2]
    g = h // hkv
    nb = s // BLOCK
    qb = q.reshape(b, nb, BLOCK, hkv, g, dh)

    def key_windows(t):
        tp = jnp.pad(t, ((0, 0), (BLOCK, BLOCK), (0, 0), (0, 0))).reshape(b, nb + 2, BLOCK, hkv, dh)
        return jnp.concatenate([tp[:, :-2], tp[:, 1:-1], tp[:, 2:]], axis=2)

    kw, vw = key_windows(k), key_windows(v)
    scores = jnp.einsum('bnqkgd,bnckd->bnkgqc', qb, kw).astype(F32) * (dh ** -0.5)
    qi = jnp.arange(BLOCK)[:, None]
    ci = jnp.arange(3 * BLOCK)[None, :]
    band = jnp.abs(ci - BLOCK - qi) <= WINDOW
    kpos = jnp.arange(nb)[:, None] * BLOCK - BLOCK + jnp.arange(3 * BLOCK)[None, :]
    valid = (kpos >= 0) & (kpos < s)
    mask = band[None, :, :] & valid[:, None, :]
    scores = jnp.where(mask[None, :, None, None], scores, -jnp.inf)
    sink_logit = jnp.broadcast_to(sink.astype(F32).reshape(1, 1, hkv, g, 1, 1), scores.shape[:-1] + (1,))
    probs = jax.nn.softmax(jnp.concatenate([scores, sink_logit], axis=-1), axis=-1)[..., :-1]
    out = jnp.einsum('bnkgqc,bnckd->bnqkgd', probs.astype(v.dtype), vw)
    return out.reshape(b, s, h * dh)


def diff_attention(q, k, v, lam, norm_g, lam_init):
    b, s, h, _, d = q.shape
    nb = s // BLOCK
    qb = jnp.moveaxis(q.reshape(b, nb, BLOCK, h, 2, d), 1, 0)
    scale = d ** -0.5

    def block(qblk):
        sc = jnp.einsum('bqhtd,bkhtd->bhtqk', qblk, k).astype(F32) * scale
        p = jax.nn.softmax(sc, axis=-1)
        w = p[:, :, 0] - lam * p[:, :, 1]
        return jnp.einsum('bhqk,bkhe->bqhe', w.astype(v.dtype), v)

    o = jnp.moveaxis(lax.map(block, qb), 0, 1).reshape(b, s, h, 2 * d)
    of = o.astype(F32)
    of = of * lax.rsqrt(jnp.mean(of * of, axis=-1, keepdims=True) + EPS) * norm_g.astype(F32) * (1.0 - lam_init)
    return of.astype(v.dtype).reshape(b, s, h * 2 * d)


def ssd_scan(x, dt, a_coef, bm, cm):
    b, s, h, p = x.shape
    g, n = bm.shape[2], bm.shape[3]
    hg = h // g
    nc = s // SSM_CHUNK
    L = SSM_CHUNK
    xdt = (x.astype(F32) * dt[..., None]).reshape(b, nc, L, g, hg, p)
    a = (dt * a_coef).reshape(b, nc, L, g, hg)
    bc = bm.astype(F32).reshape(b, nc, L, g, n)
    cc = cm.astype(F32).reshape(b, nc, L, g, n)
    a_cum = jnp.cumsum(a, axis=2)
    seg = a_cum[:, :, :, None] - a_cum[:, :, None, :]
    causal = (jnp.arange(L)[:, None] >= jnp.arange(L)[None, :])[None, None, :, :, None, None]
    decay = jnp.where(causal, jnp.exp(jnp.where(causal, seg, 0.0)), 0.0)
    cb = jnp.einsum('bclgn,bcsgn->bclsg', cc, bc)
    y_diag = jnp.einsum('bclsg,bclsgh,bcsghp->bclghp', cb, decay, xdt)
    decay_to_end = jnp.exp(a_cum[:, :, -1:] - a_cum)
    states = jnp.einsum('bclgn,bclgh,bclghp->bcghpn', bc, decay_to_end, xdt)
    chunk_decay = jnp.exp(a_cum[:, :, -1])

    def step(carry, inp):
        st, dec = inp
        return carry * dec[..., None, None] + st, carry

    init = jnp.zeros((b, g, hg, p, n), F32)
    _, prev = lax.scan(step, init, (jnp.moveaxis(states, 1, 0), jnp.moveaxis(chunk_decay, 1, 0)))
    prev = jnp.moveaxis(prev, 0, 1)
    y_off = jnp.einsum('bclgn,bcghpn,bclgh->bclghp', cc, prev, jnp.exp(a_cum))
    return (y_diag + y_off).reshape(b, s, h, p)


def mamba2_bidir(z, xbc, dt_f, dt_b, conv_w, conv_b, dtb_f, dtb_b, alog_f, alog_b, d_skip, norm_g):
    b, s, _ = z.shape
    xbc = jax.nn.silu(depthwise_conv(xbc, conv_w, conv_b))
    xs, bm, cm = split_cols(xbc, (SSM_INNER, SSM_GROUPS * SSM_STATE, SSM_GROUPS * SSM_STATE))
    xs = xs.reshape(b, s, SSM_HEADS, SSM_HEAD_DIM)
    bm = bm.reshape(b, s, SSM_GROUPS, SSM_STATE)
    cm = cm.reshape(b, s, SSM_GROUPS, SSM_STATE)
    dtf = jax.nn.softplus(dt_f.astype(F32) + dtb_f.astype(F32))
    dtbk = jax.nn.softplus(dt_b.astype(F32) + dtb_b.astype(F32))
    a_f = -jnp.exp(alog_f.astype(F32))
    a_b = -jnp.exp(alog_b.astype(F32))
    flip = lambda t: jnp.flip(t, axis=1)
    y_f = ssd_scan(xs, dtf, a_f, bm, cm)
    y_b = flip(ssd_scan(flip(xs), flip(dtbk), a_b, flip(bm), flip(cm)))
    y = y_f + y_b + xs.astype(F32) * d_skip.astype(F32)[:, None]
    y = y.reshape(b, s, SSM_INNER) * jax.nn.silu(z.astype(F32))
    return rmsnorm(y, norm_g).astype(z.dtype)


def even_layer(h, cos, sin, w_in, conv_w, conv_b, ln_g, ln_b, sink, w_out):
    b, s, _ = h.shape
    u = h @ w_in
    glu, a_gate, q, k, v, b_gate = split_cols(u, EVEN_SPLITS)
    a_val, a_glu = jnp.split(glu, 2, axis=-1)
    a = a_val * jax.nn.sigmoid(a_glu)
    a = layernorm(depthwise_conv(a, conv_w, conv_b), ln_g, ln_b)
    a = jax.nn.silu(a) * jax.nn.silu(a_gate)
    q = apply_rope(q.reshape(b, s, SWA_HEADS, HEAD_DIM), cos, sin)
    k = apply_rope(k.reshape(b, s, SWA_KV_HEADS, HEAD_DIM), cos, sin)
    v = v.reshape(b, s, SWA_KV_HEADS, HEAD_DIM)
    o = windowed_gqa(q, k, v, sink) * jax.nn.silu(b_gate)
    return jnp.concatenate([a, o], axis=-1) @ w_out


def odd_layer(h, cos, sin, lam_init, w_in, lq1, lk1, lq2, lk2, diff_g, sconv_w, sconv_b,
              dtb_f, dtb_b, alog_f, alog_b, d_skip, ssm_g, w_out):
    b, s, _ = h.shape
    u = h @ w_in
    q, k, v, c_gate, z, xbc, dt_f, dt_b = split_cols(u, ODD_SPLITS)
    q = apply_rope(q.reshape(b, s, 2 * DIFF_HEADS, HEAD_DIM), cos, sin).reshape(b, s, DIFF_HEADS, 2, HEAD_DIM)
    k = apply_rope(k.reshape(b, s, 2 * DIFF_HEADS, HEAD_DIM), cos, sin).reshape(b, s, DIFF_HEADS, 2, HEAD_DIM)
    v = v.reshape(b, s, DIFF_HEADS, 2 * HEAD_DIM)
    lam = (jnp.exp(jnp.sum(lq1.astype(F32) * lk1.astype(F32)))
           - jnp.exp(jnp.sum(lq2.astype(F32) * lk2.astype(F32))) + lam_init)
    c = diff_attention(q, k, v, lam, diff_g, lam_init) * jax.nn.silu(c_gate)
    d = mamba2_bidir(z, xbc, dt_f, dt_b, sconv_w, sconv_b, dtb_f, dtb_b, alog_f, alog_b, d_skip, ssm_g)
    return jnp.concatenate([c, d.astype(c.dtype)], axis=-1) @ w_out


def trunk(x, norm_g, w_in0, conv_w, conv_b, conv_ln_g, conv_ln_b, sink, w_out0,
          w_in1, lambda_q1, lambda_k1, lambda_q2, lambda_k2, diff_norm_g, ssm_conv_w, ssm_conv_b,
          dt_bias_f, dt_bias_b, a_log_f, a_log_b, d_skip, ssm_norm_g, w_out1, final_norm_g):
    cos, sin = rope_tables(x.shape[1])
    for layer in range(DEPTH):
        i = layer // 2
        h = rmsnorm(x, norm_g[layer])
        if layer % 2 == 0:
            x = x + even_layer(h, cos, sin, w_in0[i], conv_w[i], conv_b[i], conv_ln_g[i], conv_ln_b[i],
                               sink[i], w_out0[i])
        else:
            lam_init = 0.8 - 0.6 * math.exp(-0.3 * layer)
            x = x + odd_layer(h, cos, sin, lam_init, w_in1[i], lambda_q1[i], lambda_k1[i], lambda_q2[i],
                              lambda_k2[i], diff_norm_g[i], ssm_conv_w[i], ssm_conv_b[i], dt_bias_f[i],
                              dt_bias_b[i], a_log_f[i], a_log_b[i], d_skip[i], ssm_norm_g[i], w_out1[i])
    return rmsnorm(x, final_norm_g)


def setup_inputs(seed: int = 0) -> dict:
    key = jax.random.key(seed)
    ks = jax.random.split(key, 32)
    nrm = jax.random.normal

    def dt_bias(k):
        u = jax.random.uniform(k, (N_ODD, SSM_HEADS), F32)
        dt = jnp.exp(u * (math.log(0.1) - math.log(0.001)) + math.log(0.001))
        return dt + jnp.log(-jnp.expm1(-dt))

    return {
        "x_prompt": nrm(ks[0], (BATCH, SEQ, D_MODEL), F32),
        "x_sample": nrm(ks[1], (DEC_BATCH, DEC_SEQ, D_MODEL), F32),
        "norm_g": 1.0 + 0.02 * nrm(ks[2], (DEPTH, D_MODEL), F32),
        "w_in0": nrm(ks[3], (N_EVEN, D_MODEL, IN_EVEN), F32) * D_MODEL ** -0.5,
        "conv_w": nrm(ks[4], (N_EVEN, CONV_WIDTH, CONV_CH), F32) * CONV_WIDTH ** -0.5,
        "conv_b": 0.02 * nrm(ks[5], (N_EVEN, CONV_CH), F32),
        "conv_ln_g": 1.0 + 0.02 * nrm(ks[6], (N_EVEN, CONV_CH), F32),
        "conv_ln_b": 0.02 * nrm(ks[7], (N_EVEN, CONV_CH), F32),
        "sink": 0.5 * nrm(ks[8], (N_EVEN, SWA_HEADS), F32),
        "w_out0": nrm(ks[9], (N_EVEN, OUT_EVEN, D_MODEL), F32) * OUT_EVEN ** -0.5,
        "w_in1": nrm(ks[10], (N_ODD, D_MODEL, IN_ODD), F32) * D_MODEL ** -0.5,
        "lambda_q1": 0.1 * nrm(ks[11], (N_ODD, HEAD_DIM), F32),
        "lambda_k1": 0.1 * nrm(ks[12], (N_ODD, HEAD_DIM), F32),
        "lambda_q2": 0.1 * nrm(ks[13], (N_ODD, HEAD_DIM), F32),
        "lambda_k2": 0.1 * nrm(ks[14], (N_ODD, HEAD_DIM), F32),
        "diff_norm_g": 1.0 + 0.02 * nrm(ks[15], (N_ODD, 2 * HEAD_DIM), F32),
        "ssm_conv_w": nrm(ks[16], (N_ODD, SSM_CONV, SSM_CONV_DIM), F32) * SSM_CONV ** -0.5,
        "ssm_conv_b": 0.02 * nrm(ks[17], (N_ODD, SSM_CONV_DIM), F32),
        "dt_bias_f": dt_bias(ks[18]),
        "dt_bias_b": dt_bias(ks[19]),
        "a_log_f": jnp.log(jax.random.uniform(ks[20], (N_ODD, SSM_HEADS), F32, 1.0, 16.0)),
        "a_log_b": jnp.log(jax.random.uniform(ks[21], (N_ODD, SSM_HEADS), F32, 1.0, 16.0)),
        "d_skip": 1.0 + 0.02 * nrm(ks[22], (N_ODD, SSM_HEADS), F32),
        "ssm_norm_g": 1.0 + 0.02 * nrm(ks[23], (N_ODD, SSM_INNER), F32),
        "w_out1": nrm(ks[24], (N_ODD, OUT_ODD, D_MODEL), F32) * OUT_ODD ** -0.5,
        "final_norm_g": 1.0 + 0.02 * nrm(ks[25], (D_MODEL,), F32),
    }


def reference(x_prompt, x_sample, norm_g, w_in0, conv_w, conv_b, conv_ln_g, conv_ln_b, sink, w_out0,
              w_in1, lambda_q1, lambda_k1, lambda_q2, lambda_k2, diff_norm_g, ssm_conv_w, ssm_conv_b,
              dt_bias_f, dt_bias_b, a_log_f, a_log_b, d_skip, ssm_norm_g, w_out1, final_norm_g):
    y_prompt = trunk(x_prompt, norm_g, w_in0, conv_w, conv_b, conv_ln_g, conv_ln_b, sink, w_out0,
                     w_in1, lambda_q1, lambda_k1, lambda_q2, lambda_k2, diff_norm_g, ssm_conv_w, ssm_conv_b,
                     dt_bias_f, dt_bias_b, a_log_f, a_log_b, d_skip, ssm_norm_g, w_out1, final_norm_g)
    y_sample = trunk(x_sample, norm_g, w_in0, conv_w, conv_b, conv_ln_g, conv_ln_b, sink, w_out0,
                     w_in1, lambda_q1, lambda_k1, lambda_q2, lambda_k2, diff_norm_g, ssm_conv_w, ssm_conv_b,
                     dt_bias_f, dt_bias_b, a_log_f, a_log_b, d_skip, ssm_norm_g, w_out1, final_norm_g)
    return (y_prompt, y_sample)
```

```python
import math
from contextlib import ExitStack
import numpy as np
import concourse.bass as bass
import concourse.mybir as mybir
from concourse.bass_utils import run_bass_kernel_spmd

F32 = mybir.dt.float32
BF16 = mybir.dt.bfloat16
AF = mybir.ActivationFunctionType
ALU = mybir.AluOpType

T = 512
H = 128
TW = T + 2 * H
EPS = 1e-6
LAM_INIT = 0.8 - 0.6 * math.exp(-0.3 * 1)
STRICT_SAME_ENGINE = True


class Buf:
    __slots__ = ("w", "r")

    def __init__(self):
        self.w = None
        self.r = []


class Tl:
    def __init__(self, t, nsub=1):
        self.t = t
        self.b = [Buf() for _ in range(nsub)]


class Sched:
    def __init__(self, nc, es):
        self.nc = nc
        self.es = es
        self.eng = {"pe": nc.tensor, "act": nc.scalar, "dve": nc.vector, "pool": nc.gpsimd, "sp": nc.sync}
        self.semobj = {}
        self.cnt = {}
        self.cur = {}
        self.gen = 0
        self.new_engine_sems()
        self.known = {e: {} for e in self.eng}
        self.pending_pe = False
        self.grp = None
        self.tag = None
        self.prog = {e: [] for e in self.eng}
        self.pend = {e: [] for e in self.eng}

    def new_engine_sems(self):
        self.gen += 1
        for e in ["pe", "act", "dve", "pool"]:
            key = f"{e}.{self.gen}"
            self.semobj[key] = self.es.enter_context(self.nc.semaphore("s_" + e + str(self.gen)))
            self.cnt[key] = 0
            self.cur[e] = key

    def _need(self, e, reads, writes):
        need = {}

        def add(tok, kind):
            if tok is None:
                return
            k, v = tok
            if k == self.cur.get(e):
                if e == "pe":
                    return
                if kind != "raw" and not STRICT_SAME_ENGINE:
                    return
            if need.get(k, 0) < v:
                need[k] = v

        for b in reads:
            add(b.w, "raw")
        for b in writes:
            add(b.w, "waw")
            for tk in b.r:
                add(tk, "war")
        return need

    def _emit_waits(self, e, need):
        kn = self.known[e]
        for k, v in need.items():
            if kn.get(k, 0) >= v:
                continue
            self.eng[e].wait_ge(self.semobj[k], v)
            self.pend[e].append((k, v))
            kn[k] = v

    def _update(self, tok, reads, writes):
        for b in reads:
            b.r.append(tok)
            if len(b.r) > 24:
                mx = {}
                for k, v in b.r:
                    if mx.get(k, 0) < v:
                        mx[k] = v
                b.r = list(mx.items())
        for b in writes:
            b.w = tok
            b.r = []

    def op(self, e, fn, reads=(), writes=(), inc=True):
        need = self._need(e, reads, writes)
        self._emit_waits(e, need)
        inst = fn()
        if self.tag is not None:
            inst.annotate(self.tag)
        ck = self.cur[e]
        self.prog[e].append((self.pend[e], ck if inc else None, 1))
        self.pend[e] = []
        if inc:
            self.cnt[ck] += 1
            inst.then_inc(self.semobj[ck], 1)
            tok = (ck, self.cnt[ck])
            if e == "pe":
                self.pending_pe = False
        else:
            assert e == "pe"
            tok = (ck, self.cnt[ck] + 1)
            self.pending_pe = True
        self._update(tok, reads, writes)
        return inst

    def dma(self, q, out, in_, reads, writes, sname):
        if sname not in self.semobj:
            self.semobj[sname] = self.es.enter_context(self.nc.semaphore("d_" + sname))
            self.cnt[sname] = 0
        need = self._need(q, reads, writes)
        self._emit_waits(q, need)
        self.eng[q].dma_start(out=out, in_=in_).then_inc(self.semobj[sname], 16)
        self.prog[q].append((self.pend[q], sname, 16))
        self.pend[q] = []
        self.cnt[sname] += 16
        tok = (sname, self.cnt[sname])
        self._update(tok, reads, writes)
        if self.grp is not None:
            self.grp.append((tok, list(reads), list(writes)))

    def gbegin(self):
        self.grp = []

    def gend(self):
        g, self.grp = self.grp, None
        final = {}
        for (k, v), _, _ in g:
            final[k] = max(final.get(k, 0), v)
        for (k, v), reads, writes in g:
            ft = (k, final[k])
            for b in writes:
                if b.w is not None and b.w[0] == k:
                    b.w = ft
            for b in reads:
                b.r = [ft if tk[0] == k else tk for tk in b.r]

    def check_deadlock(self):
        prog = {e: list(p) for e, p in self.prog.items()}
        for e in prog:
            if self.pend[e]:
                prog[e].append((self.pend[e], None, 0))
        val = {}
        ptr = {e: 0 for e in prog}
        progress = True
        while progress:
            progress = False
            for e, p in prog.items():
                while ptr[e] < len(p):
                    waits, k, amt = p[ptr[e]]
                    if all(val.get(wk, 0) >= wv for wk, wv in waits):
                        if k is not None:
                            val[k] = val.get(k, 0) + amt
                        ptr[e] += 1
                        progress = True
                    else:
                        break
        stuck = {e: (ptr[e], len(p), p[ptr[e]][0]) for e, p in prog.items() if ptr[e] < len(p)}
        if stuck:
            raise RuntimeError(f"DEADLOCK: {stuck} vals={ {k: v for k, v in val.items()} }")
        return {e: len(p) for e, p in prog.items()}

    def barrier(self):
        assert not self.pending_pe
        for e in self.eng:
            need = {k: v for k, v in self.cnt.items() if v > 0 and k != self.cur.get(e)}
            self._emit_waits(e, need)
        self.new_engine_sems()


class StopPhase(Exception):
    pass


def build(jobs, dbg=False, LV=9, SUB=99):
    def ck(k):
        if SUB == k:
            raise StopPhase()
    nc = bass.Bass("TRN2", target_bir_lowering=False)
    NJ = len(jobs)

    def din(name, shape, dt=F32):
        return nc.dram_tensor(name, list(shape), dt, kind="ExternalInput").ap()

    def dscr(name, shape, dt):
        return nc.dram_tensor(name, list(shape), dt, kind="ExternalOutput" if dbg else "Internal").ap()

    xt_in, cs_in, kv_in, hf_in, kf_in, y_out = [], [], [], [], [], []
    x1T, QT, KT, VT, CG, ZS, BCT, XB, DTA, PRV = [], [], [], [], [], [], [], [], [], []
    CDd = []
    for j, jb in enumerate(jobs):
        n, nq = jb["n"], jb["nq"]
        xt_in.append(din(f"xt{j}", [n, 128, 8, TW]))
        cs_in.append(din(f"cs{j}", [n, 128, 2, TW]))
        kv_in.append(din(f"kv{j}", [n, 128, 2]))
        hf_in.append(din(f"hf{j}", [n, 128, 2]))
        kf_in.append(din(f"kf{j}", [128, 2, n * 4]))
        y_out.append(nc.dram_tensor(f"yt{j}", [nq, 128, 8, T], F32, kind="ExternalOutput").ap())
        x1T.append(dscr(f"x1T{j}", [n, 128, 8, T], F32))
        QT.append(dscr(f"QT{j}", [nq, 128, 4, T], BF16))
        KT.append(dscr(f"KT{j}", [n, 128, 4, T], BF16))
        VT.append(dscr(f"VT{j}", [n, 128, 4, 512], BF16))
        CG.append(dscr(f"CG{j}", [nq, 128, 4, T], BF16))
        ZS.append(dscr(f"ZS{j}", [nq, 128, 4, 512], BF16))
        BCT.append(dscr(f"BCT{j}", [nq, 128, 4, T], BF16))
        XB.append(dscr(f"XB{j}", [n, 128, 4, 768], BF16))
        DTA.append(dscr(f"DTA{j}", [n, 128, 4, 32], F32))
        PRV.append(dscr(f"PRV{j}", [2, n * 4, 128, 512], BF16))
        if dbg:
            CDd.append(dscr(f"CDd{j}", [nq, 128, 8, T], BF16))
    w_in0 = din("w_in0", [1024, 2816])
    w_out0 = din("w_out0", [1024, 1024])
    w_in1 = din("w_in1", [1024, 3600])
    w_out1 = din("w_out1", [1024, 1024])
    colp_in = din("colp", [128, 64])
    cw_in = din("cw", [128, 4, 31])
    scw_in = din("scw", [128, 8, 5])
    rowp_in = din("rowp", [1, 1024])
    cmat_in = din("cmat", [128, 9, 128])
    cmask_in = din("cmask", [128, 4, 512])

    es = ExitStack()
    S = Sched(nc, es)

    def sbt(st, name, shape, dt, nsub=1):
        return Tl(st.enter_context(nc.sbuf_tensor("s_" + name, list(shape), dt)), nsub)

    def pst(st, name, shape, dt):
        return Tl(st.enter_context(nc.psum_tensor("p_" + name, list(shape), dt)), 1)

    dbuf = {}

    def DB(name, j, i):
        key = (name, j, i)
        if key not in dbuf:
            dbuf[key] = Buf()
        return dbuf[key]

    colp = sbt(es, "colp", [128, 64], F32)
    cw = sbt(es, "cw", [128, 4, 31], F32)
    scw = sbt(es, "scw", [128, 8, 5], F32)
    rowp = sbt(es, "rowp", [128, 512], F32)
    cmat = sbt(es, "cmat", [128, 9, 128], F32)
    onec = sbt(es, "onec", [128, 1], F32)
    identb = sbt(es, "identb", [128, 128], BF16)
    onesb = sbt(es, "onesb", [128, 128], BF16)
    maskLR = sbt(es, "maskLR", [128, 2, 512], BF16)
    mbias = sbt(es, "mbias", [128, 2, 512], BF16)
    epsc = sbt(es, "epsc", [128, 1], F32)
    esink = sbt(es, "esink", [128, 4], F32)
    abc = sbt(es, "abc", [128, 16], F32)
    neglam = sbt(es, "neglam", [128, 1], F32)
    lamt = sbt(es, "lamt", [128, 4], F32)
    gdiff = sbt(es, "gdiff", [128, 1], F32)
    identD = sbt(es, "identD", [128, 8, 128], BF16)
    junk = sbt(es, "junk", [128, 64], F32)
    es_tmp = ExitStack()
    cmask = sbt(es_tmp, "cmask", [128, 4, 512], F32)

    S.gbegin()
    S.dma("sp", colp.t[:], colp_in, [], colp.b, "c0")
    S.dma("sp", cw.t[:], cw_in, [], cw.b, "c0")
    S.dma("sp", scw.t[:], scw_in, [], scw.b, "c0")
    S.dma("sp", rowp.t[:], rowp_in[:, 0:512].partition_broadcast(128), [], rowp.b, "c0")
    S.dma("sp", cmat.t[:], cmat_in, [], cmat.b, "c0")
    S.dma("sp", cmask.t[:], cmask_in, [], cmask.b, "c0")
    S.gend()
    S.op("dve", lambda: nc.vector.memset(onec.t[:], 1.0), [], onec.b)
    identf = cmat.t[:, 0, :]
    onesf = cmat.t[:, 1, :]
    Rf = cmat.t[:, 2, :]
    triD = [cmat.t[:, 3, :], cmat.t[:, 4, :]]
    S.op("dve", lambda: nc.vector.tensor_copy(out=identb.t[:], in_=identf), cmat.b, identb.b)
    S.op("dve", lambda: nc.vector.tensor_copy(out=onesb.t[:], in_=onesf), cmat.b, onesb.b)
    S.op("dve", lambda: nc.vector.tensor_copy(out=maskLR.t[:], in_=cmask.t[:, 0:2, :]), cmask.b, maskLR.b)
    S.op("dve", lambda: nc.vector.tensor_copy(out=mbias.t[:], in_=cmask.t[:, 2:4, :]), cmask.b, mbias.b)
    S.op("dve", lambda: nc.vector.memset(epsc.t[:], EPS), [], epsc.b)
    S.op("act", lambda: nc.scalar.activation(out=esink.t[:], in_=colp.t[:, 36:40], func=AF.Exp), colp.b, esink.b)
    S.op("act", lambda: nc.scalar.activation(out=abc.t[:], in_=rowp.t[:, 272:288], func=AF.Exp), rowp.b, abc.b)
    S.op("dve", lambda: nc.vector.tensor_scalar(out=abc.t[:], in0=abc.t[:], scalar1=-1.0, scalar2=None, op0=ALU.mult),
         abc.b, abc.b)
    S.op("dve", lambda: nc.vector.tensor_tensor(out=junk.t[:], in0=rowp.t[:, 0:64], in1=rowp.t[:, 64:128], op=ALU.mult),
         rowp.b, junk.b)
    S.op("dve", lambda: nc.vector.tensor_reduce(out=lamt.t[:, 0:1], in_=junk.t[:], axis=mybir.AxisListType.X, op=ALU.add),
         junk.b, lamt.b)
    S.op("dve", lambda: nc.vector.tensor_tensor(out=junk.t[:], in0=rowp.t[:, 128:192], in1=rowp.t[:, 192:256], op=ALU.mult),
         rowp.b + lamt.b, junk.b)
    S.op("dve", lambda: nc.vector.tensor_reduce(out=lamt.t[:, 1:2], in_=junk.t[:], axis=mybir.AxisListType.X, op=ALU.add),
         junk.b, lamt.b)
    S.op("act", lambda: nc.scalar.activation(out=lamt.t[:, 2:4], in_=lamt.t[:, 0:2], func=AF.Exp), lamt.b, lamt.b)
    S.op("dve", lambda: nc.vector.tensor_tensor(out=neglam.t[:], in0=lamt.t[:, 3:4], in1=lamt.t[:, 2:3], op=ALU.subtract),
         lamt.b, neglam.b)
    S.op("dve", lambda: nc.vector.tensor_scalar(out=neglam.t[:], in0=neglam.t[:], scalar1=-LAM_INIT, scalar2=None,
                                                op0=ALU.add), neglam.b, neglam.b)
    S.op("dve", lambda: nc.vector.tensor_scalar(out=gdiff.t[:], in0=colp.t[:, 48:49], scalar1=1.0 - LAM_INIT, scalar2=None,
                                                op0=ALU.mult), colp.b, gdiff.b)
    for h in range(8):
        S.op("dve", lambda h=h: nc.vector.tensor_scalar(out=identD.t[:, h, :], in0=identf, scalar1=rowp.t[:, 288 + h:289 + h],
                                                        scalar2=None, op0=ALU.mult), cmat.b + rowp.b, identD.b)
    S.barrier()
    es_tmp.close()

    def mm(out_ap, lhsT, rhs, start, stop, reads, writes, inc):
        S.op("pe", lambda: nc.tensor.matmul(out_ap, lhsT=lhsT, rhs=rhs, start=start, stop=stop), reads, writes, inc=inc)

    def load_weight(dst, src, ncols, stage, k_chunks=8):
        for k in range(k_chunks):
            sl = k % 2
            st_ap = stage.t[:, sl * 4:(sl + 1) * 4, :].rearrange("p a b -> p (a b)")[:, 0:ncols]
            sb = stage.b[sl * 4:(sl + 1) * 4]
            half = ncols // 2
            S.gbegin()
            S.dma("sp", st_ap[:, 0:half], src[k * 128:(k + 1) * 128, 0:half], [], sb, f"wst{sl}")
            S.dma("sp", st_ap[:, half:ncols], src[k * 128:(k + 1) * 128, half:ncols], [], sb, f"wst{sl}")
            S.gend()
            if k % 2 == 0:
                S.op("act", lambda: nc.scalar.copy(out=dst.t[:, k, :], in_=st_ap), sb, dst.b)
            else:
                S.op("dve", lambda: nc.vector.tensor_copy(out=dst.t[:, k, :], in_=st_ap), sb, dst.b)

    def rmsnorm_fm(xsrc, xb, width, ranges, gcol0, hT, sq, psA, psB, rstd):
        pss = [psA, psB]
        for c in range(8):
            s = sq[c % 2]
            S.op("act", lambda: nc.scalar.activation(out=s.t[:, 0:width], in_=xsrc(c), func=AF.Square), [xb[c]], s.b)
            for i, (a, b_) in enumerate(ranges):
                mm(pss[i].t[:, 0:b_ - a], onesb.t[:], s.t[:, a:b_], c == 0, c == 7, s.b + onesb.b, pss[i].b, i == len(ranges) - 1)
        for i, (a, b_) in enumerate(ranges):
            S.op("act", lambda: nc.scalar.activation(out=rstd.t[:, a:b_], in_=pss[i].t[:, 0:b_ - a], func=AF.Ln,
                                                     bias=epsc.t[:], scale=1.0 / 1024), pss[i].b + epsc.b, rstd.b)
        S.op("act", lambda: nc.scalar.activation(out=rstd.t[:, 0:width], in_=rstd.t[:, 0:width], func=AF.Exp, scale=-0.5), rstd.b, rstd.b)
        for c in range(8):
            S.op("dve", lambda: nc.vector.scalar_tensor_tensor(out=hT.t[:, c, :], in0=xsrc(c), scalar=colp.t[:, gcol0 + c:gcol0 + c + 1],
                                                               in1=rstd.t[:, 0:width], op0=ALU.mult, op1=ALU.mult),
                 [xb[c]] + rstd.b + colp.b, hT.b)

    def rope(ps, width, cos_ap, sin_ap, cs_b, qf, rot_ps, tmp, out_ap, out_b, outs=None):
        S.op("act", lambda: nc.scalar.copy(out=qf.t[:, 0:width], in_=ps.t[:, 0:width]), ps.b, qf.b)
        mm(rot_ps.t[:, 0:width], Rf, qf.t[:, 0:width], True, True, qf.b + cmat.b, rot_ps.b, True)
        S.op("dve", lambda: nc.vector.tensor_tensor(out=tmp.t[:, 0:width], in0=rot_ps.t[:, 0:width], in1=sin_ap, op=ALU.mult),
             rot_ps.b + cs_b, tmp.b)
        S.op("dve", lambda: nc.vector.tensor_tensor(out=qf.t[:, 0:width], in0=qf.t[:, 0:width], in1=cos_ap, op=ALU.mult),
             qf.b + cs_b, qf.b)
        if outs is None:
            S.op("dve", lambda: nc.vector.tensor_tensor(out=out_ap, in0=qf.t[:, 0:width], in1=tmp.t[:, 0:width], op=ALU.add),
                 qf.b + tmp.b, out_b)
        else:
            for psl, oap in outs:
                S.op("dve", lambda: nc.vector.tensor_tensor(out=oap, in0=qf.t[psl, 0:width], in1=tmp.t[psl, 0:width], op=ALU.add),
                     qf.b + tmp.b, out_b)

    with ExitStack() as ph:
        w0 = sbt(ph, "w0", [128, 8, 2816], BF16)
        wo = sbt(ph, "wo", [128, 8, 1024], BF16)
        diag = sbt(ph, "diag", [128, 124, 128], BF16)
        xt = sbt(ph, "xt", [128, 8, TW], F32, 8)
        cs = sbt(ph, "cs", [128, 2, TW], F32)
        kvf2 = [sbt(ph, f"kvf{i}", [128, 2], F32) for i in range(2)]
        sq = [sbt(ph, f"sq{i}", [128, TW], BF16) for i in range(2)]
        rstd = sbt(ph, "rstd", [128, TW], F32)
        hT = sbt(ph, "hT", [128, 8, TW], BF16)
        a_sb = sbt(ph, "a_sb", [128, 4, 544], BF16, 4)
        sg = [sbt(ph, f"sg{i}", [128, 512], BF16) for i in range(2)]
        co = sbt(ph, "co", [128, 4, 512], F32, 4)
        cobf0 = sbt(ph, "cobf0", [128, 512], BF16)
        cobf = [cobf0, cobf0]
        co20 = sbt(ph, "co20", [128, 512], BF16)
        co2 = [co20, co20]
        mean_sb = sbt(ph, "mean_sb", [128, 512], F32)
        rs2 = sbt(ph, "rs2", [128, 512], F32)
        qf0 = sbt(ph, "qf0", [128, 512], F32)
        qf = [qf0, qf0]
        tmpf0 = sbt(ph, "tmpf0", [128, 512], F32)
        tmpf = [tmpf0, tmpf0]
        qr = sbt(ph, "qr", [128, 4, 512], BF16, 4)
        kr = sbt(ph, "kr", [128, 2, TW], BF16)
        vaug = sbt(ph, "vaug", [128, 6, 2, 128], BF16)
        sbg = sbt(ph, "sbg", [128, 4, 512], BF16, 4)
        Et = [sbt(ph, f"Et{i}", [128, 512], BF16) for i in range(3)]
        rd = tmpf0
        AO = sbt(ph, "AO", [128, 8, 512], BF16, 8)
        xc0 = sbt(ph, "xc0", [128, 2, 512], F32)
        xc2 = [xc0, xc0]
        pb = [pst(ph, f"pb{i}", [128, 512], F32) for i in range(8)]

        load_weight(w0, w_in0, 2816, xt)
        load_weight(wo, w_out0, 1024, xt)
        S.op("pool", lambda: nc.gpsimd.memset(vaug.t[:], 1.0), [], vaug.b)
        S.op("pool", lambda: nc.gpsimd.memset(kr.t[:], 0.0), [], kr.b)
        for cc in range(4):
            for jj in range(31):
                S.op("dve", lambda: nc.vector.tensor_scalar(out=diag.t[:, cc * 31 + jj, :], in0=identf, scalar1=cw.t[:, cc, jj:jj + 1],
                                                            scalar2=None, op0=ALU.mult), cmat.b + cw.b, diag.b)

        rr = [0]

        def bank():
            rr[0] = (rr[0] + 1) % 4
            return pb[rr[0]]

        tiles = [(j, t) for j, jb in enumerate(jobs if LV >= 1 else []) for t in range(jb["n"])]

        def load_norm(idx):
            j, t = tiles[idx]
            kvf = kvf2[idx % 2]
            S.gbegin()
            for c2 in range(4):
                S.dma("sp", xt.t[:, 2 * c2:2 * c2 + 2, :], xt_in[j][t, :, 2 * c2:2 * c2 + 2, :], [], xt.b[2 * c2:2 * c2 + 2], "xt")
            S.gend()
            S.dma("sp", kvf.t[:], kv_in[j][t], [], kvf.b, f"kvf{idx % 2}")
            S.tag = "norm"
            rmsnorm_fm(lambda c: xt.t[:, c, :], xt.b, TW, [(0, 512), (512, 768)], 0, hT, sq, pb[6], pb[7], rstd)

        def stage_b(idx):
            j, t = tiles[idx]
            S.dma("sp", cs.t[:], cs_in[j][t], [], cs.b, "cs")
            S.tag = "glu"
            for cc in range(4):
                for half in range(2):
                    r0 = 112 + half * 272
                    vps, gps = (pb[0], pb[1]) if (cc * 2 + half) % 2 == 0 else (pb[2], pb[3])
                    for k in range(8):
                        mm(vps.t[:, 0:272], w0.t[:, k, cc * 128:(cc + 1) * 128], hT.t[:, k, r0:r0 + 272], k == 0, k == 7,
                           w0.b + hT.b, vps.b, k == 7)
                    for k in range(8):
                        mm(gps.t[:, 0:272], w0.t[:, k, 512 + cc * 128:512 + (cc + 1) * 128], hT.t[:, k, r0:r0 + 272], k == 0, k == 7,
                           w0.b + hT.b, gps.b, k == 7)
                    sgt = sg[half]
                    S.op("act", lambda: nc.scalar.activation(out=sgt.t[:, 0:272], in_=gps.t[:, 0:272], func=AF.Sigmoid), gps.b, sgt.b)
                    S.op("dve", lambda: nc.vector.tensor_tensor(out=a_sb.t[:, cc, half * 272:(half + 1) * 272], in0=vps.t[:, 0:272],
                                                                in1=sgt.t[:, 0:272], op=ALU.mult), vps.b + sgt.b, [a_sb.b[cc]])
            S.tag = "qk"
            for cc in range(4):
                qps = bank()
                for k in range(8):
                    mm(qps.t[:], w0.t[:, k, 1536 + cc * 128:1536 + (cc + 1) * 128], hT.t[:, k, H:H + T], k == 0, k == 7, w0.b + hT.b, qps.b, k == 7)
                rope(qps, 512, cs.t[:, 0, H:H + T], cs.t[:, 1, H:H + T], cs.b, qf[cc % 2], pb[4 + cc % 2], tmpf[cc % 2], qr.t[:, cc, :], [qr.b[cc]])
            for i, (a, b_) in enumerate([(0, 512), (512, 768)]):
                kps = bank()
                for k in range(8):
                    mm(kps.t[:, 0:b_ - a], w0.t[:, k, 2048:2176], hT.t[:, k, a:b_], k == 0, k == 7, w0.b + hT.b, kps.b, k == 7)
                rope(kps, b_ - a, cs.t[:, 0, a:b_], cs.t[:, 1, a:b_], cs.b, qf[i], pb[4 + i], tmpf[i], None, kr.b,
                     outs=[(slice(0, 64), kr.t[0:64, 0, a:b_]), (slice(64, 128), kr.t[64:128, 1, a:b_])])
            S.tag = "v"
            for blk in range(6):
                vps = bank()
                for k in range(8):
                    mm(vps.t[:, 0:128], hT.t[:, k, blk * 128:(blk + 1) * 128], w0.t[:, k, 2176:2304], k == 0, k == 7, w0.b + hT.b, vps.b, k == 7)
                S.op("act", lambda: nc.scalar.copy(out=vaug.t[:, blk, 0, 0:64], in_=vps.t[:, 0:64]), vps.b, vaug.b)
                S.op("dve", lambda: nc.vector.tensor_copy(out=vaug.t[:, blk, 1, 64:128], in_=vps.t[:, 64:128]), vps.b, vaug.b)
            S.tag = "gates"
            for cc in range(4):
                gps = bank()
                for k in range(8):
                    mm(gps.t[:], w0.t[:, k, 2304 + cc * 128:2304 + (cc + 1) * 128], hT.t[:, k, H:H + T], k == 0, k == 7, w0.b + hT.b, gps.b, k == 7)
                S.op("act", lambda: nc.scalar.activation(out=sbg.t[:, cc, :], in_=gps.t[:], func=AF.Silu), gps.b, [sbg.b[cc]])
            for cc in range(4):
                gps = bank()
                for k in range(8):
                    mm(gps.t[:], w0.t[:, k, 1024 + cc * 128:1024 + (cc + 1) * 128], hT.t[:, k, H:H + T], k == 0, k == 7, w0.b + hT.b, gps.b, k == 7)
                S.op("act", lambda: nc.scalar.activation(out=AO.t[:, cc, :], in_=gps.t[:], func=AF.Silu), gps.b, [AO.b[cc]])

        def stage_c(idx):
            j, t = tiles[idx]
            kvf = kvf2[idx % 2]
            S.tag = "conv"
            for cc in range(4):
                cps = pb[4 + cc % 2]
                for jj in range(31):
                    mm(cps.t[:], diag.t[:, cc * 31 + jj, :], a_sb.t[:, cc, jj + 1:jj + 1 + 512], jj == 0, jj == 30,
                       diag.b + [a_sb.b[cc]], cps.b, jj == 30)
                S.op("act", lambda: nc.scalar.activation(out=co.t[:, cc, :], in_=cps.t[:], func=AF.Identity,
                                                         bias=colp.t[:, 24 + cc:25 + cc], scale=1.0), cps.b + colp.b, [co.b[cc]])
                S.op("act", lambda: nc.scalar.activation(out=co2[cc % 2].t[:], in_=co.t[:, cc, :], func=AF.Square), [co.b[cc]], co2[cc % 2].b)
                S.op("dve", lambda: nc.vector.tensor_copy(out=cobf[cc % 2].t[:], in_=co.t[:, cc, :]), [co.b[cc]], cobf[cc % 2].b)
                mm(pb[6].t[:], onesb.t[:], cobf[cc % 2].t[:], cc == 0, cc == 3, cobf[cc % 2].b + onesb.b, pb[6].b, True)
                mm(pb[7].t[:], onesb.t[:], co2[cc % 2].t[:], cc == 0, cc == 3, co2[cc % 2].b + onesb.b, pb[7].b, True)
            S.op("dve", lambda: nc.vector.tensor_scalar(out=mean_sb.t[:], in0=pb[6].t[:], scalar1=1.0 / 512, scalar2=None, op0=ALU.mult),
                 pb[6].b, mean_sb.b)
            S.op("dve", lambda: nc.vector.tensor_tensor(out=rs2.t[:], in0=mean_sb.t[:], in1=mean_sb.t[:], op=ALU.mult), mean_sb.b, rs2.b)
            S.op("dve", lambda: nc.vector.scalar_tensor_tensor(out=rs2.t[:], in0=pb[7].t[:], scalar=1.0 / 512, in1=rs2.t[:],
                                                               op0=ALU.mult, op1=ALU.subtract), pb[7].b + rs2.b, rs2.b)

            def ln_stats():
                S.op("act", lambda: nc.scalar.activation(out=rs2.t[:], in_=rs2.t[:], func=AF.Ln, bias=epsc.t[:], scale=1.0), rs2.b + epsc.b, rs2.b)
                S.op("act", lambda: nc.scalar.activation(out=rs2.t[:], in_=rs2.t[:], func=AF.Exp, scale=-0.5), rs2.b, rs2.b)
            ei = 0
            S.tag = "att"
            def ln_apply(cc):
                S.op("dve", lambda: nc.vector.tensor_tensor(out=co.t[:, cc, :], in0=co.t[:, cc, :], in1=mean_sb.t[:], op=ALU.subtract),
                     [co.b[cc]] + mean_sb.b, [co.b[cc]])
                S.op("dve", lambda: nc.vector.tensor_tensor(out=co.t[:, cc, :], in0=co.t[:, cc, :], in1=rs2.t[:], op=ALU.mult),
                     [co.b[cc]] + rs2.b, [co.b[cc]])
                S.op("act", lambda: nc.scalar.activation(out=co.t[:, cc, :], in_=co.t[:, cc, :], func=AF.Silu,
                                                         bias=colp.t[:, 32 + cc:33 + cc], scale=colp.t[:, 28 + cc:29 + cc]),
                     [co.b[cc]] + colp.b, [co.b[cc]])
                S.op("dve", lambda: nc.vector.tensor_tensor(out=AO.t[:, cc, :], in0=co.t[:, cc, :], in1=AO.t[:, cc, :], op=ALU.mult),
                     [co.b[cc], AO.b[cc]], [AO.b[cc]])

            pending = []
            groups = [(qb, kh) for qb in range(4) for kh in range(2)]
            sset = [[pb[0], pb[1], pb[2]], [pb[3], pb[4], pb[5]]]

            def emit_s(gi):
                qb, kh = groups[gi]
                for jj in range(3):
                    blk = qb + jj
                    sps = sset[gi % 2][jj]
                    mm(sps.t[:], kr.t[:, kh, blk * 128:(blk + 1) * 128], qr.t[:, :, qb * 128:(qb + 1) * 128], True, True,
                       kr.b + qr.b, sps.b, True)

            emit_s(0)
            for gi, (qb, kh) in enumerate(groups):
                op_ = slice(kh * 64, (kh + 1) * 64)
                dp_ = slice((1 - kh) * 64, (2 - kh) * 64)
                ops_ = pb[6 + gi % 2]
                if gi + 1 < len(groups):
                    emit_s(gi + 1)
                Es = []
                for jj in range(3):
                    sps = sset[gi % 2][jj]
                    E = Et[jj]
                    Es.append(E)
                    S.op("act", lambda: nc.scalar.activation(out=E.t[:], in_=sps.t[:], func=AF.Exp, scale=0.125), sps.b, E.b)
                for jj in (0, 2):
                    E = Es[jj]
                    if jj == 0 and qb == 0:
                        sc = kvf.t[:, 0:1]
                    elif jj == 2 and qb == 3:
                        sc = kvf.t[:, 1:2]
                    else:
                        sc = 1.0
                    S.op("dve", lambda: nc.vector.scalar_tensor_tensor(out=E.t[:], in0=E.t[:], scalar=sc, in1=maskLR.t[:, jj // 2, :],
                                                                       op0=ALU.mult, op1=ALU.mult), E.b + maskLR.b + kvf.b, E.b)
                for jj in (1, 0, 2):
                    blk = qb + jj
                    mm(ops_.t[:], vaug.t[:, blk, kh, :], Es[jj].t[:], jj == 1, jj == 2, vaug.b + Es[jj].b, ops_.b, jj == 2)

                def normalize(ops_=ops_, op_=op_, dp_=dp_, qb=qb):
                    S.op("dve", lambda: nc.vector.tensor_tensor(out=rd.t[dp_, :].rearrange("p (g q) -> p g q", g=4),
                                                                in0=ops_.t[dp_, :].rearrange("p (g q) -> p g q", g=4),
                                                                in1=esink.t[dp_, :].unsqueeze(2).to_broadcast([64, 4, 128]), op=ALU.add),
                         ops_.b + esink.b, rd.b)
                    S.op("act", lambda: nc.scalar.activation(out=rd.t[dp_, :], in_=rd.t[dp_, :], func=AF.Ln), rd.b, rd.b)
                    S.op("act", lambda: nc.scalar.activation(out=rd.t[dp_, :], in_=rd.t[dp_, :], func=AF.Exp, scale=-1.0), rd.b, rd.b)
                    S.op("dve", lambda: nc.vector.tensor_tensor(out=rd.t[op_, :], in0=ops_.t[op_, :], in1=rd.t[dp_, :], op=ALU.mult),
                         ops_.b + rd.b, rd.b)
                    S.op("dve", lambda: nc.vector.tensor_tensor(out=AO.t[op_, 4:8, qb * 128:(qb + 1) * 128],
                                                                in0=rd.t[op_, :].rearrange("p (g q) -> p g q", g=4),
                                                                in1=sbg.t[op_, :, qb * 128:(qb + 1) * 128], op=ALU.mult),
                         rd.b + sbg.b, AO.b[4:8])

                for fn in pending:
                    fn()
                pending = [normalize]
                if gi == 1:
                    pending.append(ln_stats)
                if gi == 4:
                    for cc_ in range(4):
                        pending.append(lambda cc_=cc_: ln_apply(cc_))
            for fn in pending:
                fn()
            S.tag = "outp"
            for oc in range(8):
                xc = xc2[(oc // 2) % 2]
                if oc % 2 == 0:
                    S.dma("sp", xc.t[:], xt_in[j][t, :, oc:oc + 2, H:H + T], [], xc.b, f"xc{(oc // 2) % 2}")
                yps = pb[oc % 4]
                for ki, k in enumerate([4, 5, 6, 7, 0, 1, 2, 3]):
                    mm(yps.t[:], wo.t[:, k, oc * 128:(oc + 1) * 128], AO.t[:, k, :], ki == 0, ki == 7, wo.b + [AO.b[k]], yps.b, ki == 7)
                S.op("dve", lambda: nc.vector.tensor_tensor(out=xc.t[:, oc % 2, :], in0=yps.t[:], in1=xc.t[:, oc % 2, :], op=ALU.add),
                     yps.b + xc.b, xc.b)
                if oc % 2 == 1:
                    S.dma("pool", x1T[j][t, :, oc - 1:oc + 1, :], xc.t[:], xc.b, [DB("x1T", j, t)], f"x1w{(oc // 2) % 2}")

        if tiles:
            load_norm(0)
        for idx in range(len(tiles)):
            stage_b(idx)
            if idx + 1 < len(tiles):
                load_norm(idx + 1)
            stage_c(idx)
        S.barrier()

    W1 = 516
    with ExitStack() as ph:
        w1 = sbt(ph, "w1", [128, 8, 3600], BF16)
        diag5 = sbt(ph, "diag5", [128, 40, 128], BF16)
        x1t = sbt(ph, "x1t", [128, 8, W1], F32, 8)
        stg = sbt(ph, "stg", [128, 8, 1024], F32, 8)
        cs = sbt(ph, "cs1", [128, 2, TW], F32)
        hfl = sbt(ph, "hfl", [128, 2], F32)
        sq = [sbt(ph, f"sq1{i}", [128, W1], BF16) for i in range(2)]
        rstd = sbt(ph, "rstd1", [128, W1], F32)
        hT = sbt(ph, "hT1", [128, 8, W1], BF16)
        qf = [sbt(ph, f"qf1{i}", [128, 512], F32) for i in range(2)]
        tmpf = [sbt(ph, f"tmpf1{i}", [128, 512], F32) for i in range(2)]
        qr1 = sbt(ph, "qr1", [128, 4, 512], BF16)
        kr1 = sbt(ph, "kr1", [128, 4, 512], BF16)
        vt1 = sbt(ph, "vt1", [128, 4, 512], BF16)
        cgt = sbt(ph, "cgt", [128, 4, 512], BF16)
        zt = sbt(ph, "zt", [128, 4, 512], BF16)
        xr = sbt(ph, "xr", [128, 8, W1], BF16, 8)
        xa = sbt(ph, "xa", [128, 8, 512], BF16, 8)
        xb = sbt(ph, "xb", [128, 4, 768], BF16)
        dta = sbt(ph, "dta", [128, 4, 32], F32)
        dtmp = sbt(ph, "dtmp", [128, 16], F32)
        pb = [pst(ph, f"pc{i}", [128, 512], F32) for i in range(7)]
        ptb = pst(ph, "ptb", [128, 1024], BF16)

        load_weight(w1, w_in1, 3600, stg)
        for c8 in range(8):
            for jj in range(5):
                S.op("dve", lambda c8=c8, jj=jj: nc.vector.tensor_scalar(out=diag5.t[:, c8 * 5 + jj, :], in0=identf,
                                                                         scalar1=scw.t[:, c8, jj:jj + 1], scalar2=None, op0=ALU.mult),
                     cmat.b + scw.b, diag5.b)
        rr = [0]

        def bank():
            rr[0] = (rr[0] + 1) % 2
            return pb[rr[0]]

        tilesb = [(j, t) for j, jb in enumerate(jobs if LV >= 2 else []) for t in range(jb["n"])]

        def b_load_norm(idx):
            j, t = tilesb[idx]
            n, nq = jobs[j]["n"], jobs[j]["nq"]
            isq = t < nq
            S.gbegin()
            for c2 in range(4):
                S.dma("sp", x1t.t[:, 2 * c2:2 * c2 + 2, 2:514], x1T[j][t, :, 2 * c2:2 * c2 + 2, :], [DB("x1T", j, t)], x1t.b[2 * c2:2 * c2 + 2], "x1t")
            S.dma("sp", x1t.t[:, :, 0:2], x1T[j][(t - 1) % n, :, :, 510:512], [DB("x1T", j, (t - 1) % n)], x1t.b, "x1t")
            S.dma("sp", x1t.t[:, :, 514:516], x1T[j][(t + 1) % n, :, :, 0:2], [DB("x1T", j, (t + 1) % n)], x1t.b, "x1t")
            S.gend()
            S.dma("sp", cs.t[:], cs_in[j][t], [], cs.b, "cs")
            S.dma("sp", hfl.t[:], hf_in[j][t], [], hfl.b, "hfl")
            S.op("dve", lambda: nc.vector.tensor_scalar(out=x1t.t[:, :, 0:2], in0=x1t.t[:, :, 0:2], scalar1=hfl.t[:, 0:1], scalar2=None,
                                                        op0=ALU.mult), x1t.b + hfl.b, x1t.b)
            S.op("dve", lambda: nc.vector.tensor_scalar(out=x1t.t[:, :, 514:516], in0=x1t.t[:, :, 514:516], scalar1=hfl.t[:, 1:2], scalar2=None,
                                                        op0=ALU.mult), x1t.b + hfl.b, x1t.b)
            rmsnorm_fm(lambda c: x1t.t[:, c, :], x1t.b, W1, [(0, 258), (258, 516)], 8, hT, sq, pb[2], pb[3], rstd)

        def b_proj(idx):
            j, t = tilesb[idx]
            n, nq = jobs[j]["n"], jobs[j]["nq"]
            isq = t < nq
            cosc, sinc = cs.t[:, 0, H:H + T], cs.t[:, 1, H:H + T]
            for which, dstt, on in ((0, qr1, isq), (1, kr1, True)):
                if not on:
                    continue
                for hh in range(4):
                    qps = bank()
                    c0 = which * 512 + hh * 128
                    for k in range(8):
                        mm(qps.t[:], w1.t[:, k, c0:c0 + 128], hT.t[:, k, 2:514], k == 0, k == 7, w1.b + hT.b, qps.b, k == 7)
                    rope(qps, 512, cosc, sinc, cs.b, qf[hh % 2], pb[4 + hh % 2], tmpf[hh % 2], dstt.t[:, hh, :], dstt.b)
                if which == 0:
                    S.dma("pool", QT[j][t], qr1.t[:], qr1.b, [DB("QT", j, t)], "wq")
                else:
                    S.dma("pool", KT[j][t], kr1.t[:], kr1.b, [DB("KT", j, t)], "wk")
            for blk in range(4):
                vps = bank()
                for k in range(8):
                    mm(vps.t[:], hT.t[:, k, 2 + blk * 128:2 + (blk + 1) * 128], w1.t[:, k, 1024:1536], k == 0, k == 7, w1.b + hT.b, vps.b, k == 7)
                S.op("act", lambda: nc.scalar.copy(out=vt1.t[:, blk, :], in_=vps.t[:]), vps.b, vt1.b)
            S.dma("pool", VT[j][t], vt1.t[:], vt1.b, [DB("VT", j, t)], "wv")
            if isq:
                for blk in range(4):
                    gps = bank()
                    for k in range(8):
                        mm(gps.t[:], hT.t[:, k, 2 + blk * 128:2 + (blk + 1) * 128], w1.t[:, k, 1536:2048], k == 0, k == 7, w1.b + hT.b, gps.b, k == 7)
                    S.op("act", lambda: nc.scalar.activation(out=cgt.t[:, blk, :], in_=gps.t[:], func=AF.Silu), gps.b, cgt.b)
                S.dma("pool", CG[j][t], cgt.t[:], cgt.b, [DB("CG", j, t)], "wcg")
                for blk in range(4):
                    zps = bank()
                    for k in range(8):
                        mm(zps.t[:], hT.t[:, k, 2 + blk * 128:2 + (blk + 1) * 128], w1.t[:, k, 2048:2560], k == 0, k == 7, w1.b + hT.b, zps.b, k == 7)
                    S.op("act", lambda: nc.scalar.activation(out=zt.t[:, blk, :], in_=zps.t[:], func=AF.Silu), zps.b, zt.b)
                S.dma("pool", ZS[j][t], zt.t[:], zt.b, [DB("ZS", j, t)], "wz")
            for c8 in range(8):
                for half in range(2):
                    xps = bank()
                    for k in range(8):
                        mm(xps.t[:, 0:258], w1.t[:, k, 2560 + c8 * 128:2560 + (c8 + 1) * 128], hT.t[:, k, half * 258:(half + 1) * 258], k == 0, k == 7,
                           w1.b + hT.b, xps.b, k == 7)
                    if half == 0:
                        S.op("act", lambda: nc.scalar.copy(out=xr.t[:, c8, 0:258], in_=xps.t[:, 0:258]), xps.b, [xr.b[c8]])
                    else:
                        S.op("dve", lambda: nc.vector.tensor_copy(out=xr.t[:, c8, 258:516], in_=xps.t[:, 0:258]), xps.b, [xr.b[c8]])
            for blk in range(4):
                dps = pb[6]
                for k in range(8):
                    mm(dps.t[:, 0:16], hT.t[:, k, 2 + blk * 128:2 + (blk + 1) * 128], w1.t[:, k, 3584:3600], k == 0, k == 7, w1.b + hT.b, dps.b, k == 7)
                S.op("dve", lambda: nc.vector.tensor_tensor(out=dtmp.t[:], in0=dps.t[:, 0:16], in1=rowp.t[:, 256:272], op=ALU.add),
                     dps.b + rowp.b, dtmp.b)
                S.op("act", lambda: nc.scalar.activation(out=dtmp.t[:], in_=dtmp.t[:], func=AF.Exp), dtmp.b, dtmp.b)
                S.op("act", lambda: nc.scalar.activation(out=dta.t[:, blk, 0:16], in_=dtmp.t[:], func=AF.Ln, bias=onec.t[:], scale=1.0), dtmp.b + onec.b, dta.b)
                S.op("dve", lambda: nc.vector.tensor_tensor(out=dta.t[:, blk, 16:32], in0=dta.t[:, blk, 0:16], in1=abc.t[:], op=ALU.mult),
                     dta.b + abc.b, dta.b)
            S.dma("pool", DTA[j][t], dta.t[:], dta.b, [DB("DTA", j, t)], "wdta")

        def b_post(idx):
            j, t = tilesb[idx]
            n, nq = jobs[j]["n"], jobs[j]["nq"]
            isq = t < nq
            for c8 in range(8):
                cps = pb[4 + c8 % 2]
                for jj in range(5):
                    mm(cps.t[:], diag5.t[:, c8 * 5 + jj, :], xr.t[:, c8, jj:jj + 512], jj == 0, jj == 4, diag5.b + [xr.b[c8]], cps.b, jj == 4)
                S.op("act", lambda: nc.scalar.activation(out=xa.t[:, c8, :], in_=cps.t[:], func=AF.Silu, bias=colp.t[:, 40 + c8:41 + c8], scale=1.0),
                     cps.b + colp.b, [xa.b[c8]])
            if isq:
                S.dma("pool", BCT[j][t], xa.t[:, 4:8, :], xa.b[4:8], [DB("BCT", j, t)], "wbc")
            for blk in range(4):
                for c6 in range(6):
                    S.op("pe", lambda: nc.tensor.transpose(out=ptb.t[:, c6 * 128:(c6 + 1) * 128], in_=xa.t[:, c6, blk * 128:(blk + 1) * 128],
                                                           identity=identb.t[:]), [xa.b[c6]] + identb.b, ptb.b, inc=(c6 == 5))
                S.op("dve", lambda: nc.vector.tensor_copy(out=xb.t[:, blk, :], in_=ptb.t[:, 0:768]), ptb.b, xb.b)
            S.dma("pool", XB[j][t], xb.t[:], xb.b, [DB("XB", j, t)], "wxb")

        if tilesb:
            b_load_norm(0)
        for idx in range(len(tilesb)):
            b_proj(idx)
            if idx + 1 < len(tilesb):
                b_load_norm(idx + 1)
            b_post(idx)
        S.barrier()

    with ExitStack() as ph:
        NB = 3
        xbk = [sbt(ph, f"xbk{i}", [128, 768], BF16) for i in range(NB)]
        dtk = [sbt(ph, f"dtk{i}", [128, 32], F32) for i in range(NB)]
        kfl = sbt(ph, "kfl", [128, 2, 64], F32)
        carry = sbt(ph, "carry", [128, 512], F32)
        sm = [sbt(ph, f"sm{i}", [128, 48], F32) for i in range(2)]
        xw = [sbt(ph, f"xw{i}", [128, 512], BF16) for i in range(2)]
        pvb_ = [sbt(ph, f"pvo{i}", [128, 512], BF16) for i in range(2)]
        pss = [pst(ph, f"p2s{i}", [128, 512], F32) for i in range(2)]
        psm = [pst(ph, f"p2m{i}", [128, 16], F32) for i in range(2)]
        it = 0
        for j, jb in enumerate(jobs if LV >= 3 else []):
            n, nq = jb["n"], jb["nq"]
            NC = n * 4
            S.dma("sp", kfl.t[:, :, 0:NC], kf_in[j], [], kfl.b, "kfl")
            for d in range(2):
                order = [(nq * 4 + i) % NC for i in range(NC)] if d == 0 else list(range(NC - 1, -1, -1))
                S.op("dve", lambda: nc.vector.memset(carry.t[:], 0.0), [], carry.b)
                for c in order:
                    tt, blk = c // 4, c % 4
                    xk, dk, s_, xw_, pv_, ps_, pm_ = xbk[it % NB], dtk[it % NB], sm[it % 2], xw[it % 2], pvb_[it % 2], pss[it % 2], psm[it % 2]
                    it += 1
                    S.gbegin()
                    S.dma("sp", xk.t[:], XB[j][tt, :, blk, :], [DB("XB", j, tt)], xk.b, f"p2x{it % NB}")
                    S.dma("sp", dk.t[:], DTA[j][tt, :, blk, :], [DB("DTA", j, tt)], dk.b, f"p2x{it % NB}")
                    S.gend()
                    a_ap = dk.t[:, 16 + 8 * d:24 + 8 * d]
                    dt_ap = dk.t[:, 8 * d:8 * d + 8]
                    mm(pm_.t[:, 0:8], triD[d], a_ap, True, True, cmat.b + dk.b, pm_.b, False)
                    mm(pm_.t[:, 8:16], onesf, a_ap, True, True, cmat.b + dk.b, pm_.b, True)
                    S.op("act", lambda: nc.scalar.copy(out=s_.t[:, 0:8], in_=pm_.t[:, 8:16]), pm_.b, s_.b)
                    S.op("dve", lambda: nc.vector.tensor_tensor(out=s_.t[:, 8:16], in0=s_.t[:, 0:8], in1=pm_.t[:, 0:8], op=ALU.subtract),
                         s_.b + pm_.b, s_.b)
                    S.op("act", lambda: nc.scalar.activation(out=s_.t[:, 8:16], in_=s_.t[:, 8:16], func=AF.Exp), s_.b, s_.b)
                    S.op("act", lambda: nc.scalar.activation(out=s_.t[:, 16:24], in_=s_.t[:, 0:8], func=AF.Exp), s_.b, s_.b)
                    S.op("dve", lambda: nc.vector.tensor_tensor(out=s_.t[:, 8:16], in0=s_.t[:, 8:16], in1=dt_ap, op=ALU.mult), s_.b + dk.b, s_.b)
                    S.op("dve", lambda: nc.vector.tensor_scalar(out=s_.t[:, 16:24], in0=s_.t[:, 16:24], scalar1=kfl.t[:, d, c:c + 1], scalar2=None,
                                                                op0=ALU.mult), s_.b + kfl.b, s_.b)
                    S.op("dve", lambda: nc.vector.tensor_tensor(out=xw_.t[:].rearrange("p (h d) -> p h d", h=8),
                                                                in0=xk.t[:, 0:512].rearrange("p (h d) -> p h d", h=8),
                                                                in1=s_.t[:, 8:16].unsqueeze(2).to_broadcast([128, 8, 64]), op=ALU.mult),
                         xk.b + s_.b, xw_.b)
                    for g in range(2):
                        mm(ps_.t[:, g * 256:(g + 1) * 256], xk.t[:, 512 + g * 128:512 + (g + 1) * 128], xw_.t[:, g * 256:(g + 1) * 256], True, True,
                           xk.b + xw_.b, ps_.b, g == 1)
                    if c < nq * 4 or dbg:
                        S.op("act", lambda: nc.scalar.activation(out=pv_.t[:], in_=carry.t[:], func=AF.Identity, scale=kfl.t[:, d, c:c + 1]),
                             carry.b + kfl.b, pv_.b)
                        S.dma("pool", PRV[j][d, c], pv_.t[:], pv_.b, [DB("PRV", j, (d, c))], f"w2{it % 2}")
                    S.op("dve", lambda: nc.vector.tensor_tensor(out=carry.t[:].rearrange("p (h d) -> p h d", h=8),
                                                                in0=carry.t[:].rearrange("p (h d) -> p h d", h=8),
                                                                in1=s_.t[:, 16:24].unsqueeze(2).to_broadcast([128, 8, 64]), op=ALU.mult),
                         carry.b + s_.b, carry.b)
                    S.op("dve", lambda: nc.vector.tensor_tensor(out=carry.t[:], in0=carry.t[:], in1=ps_.t[:], op=ALU.add), carry.b + ps_.b, carry.b)
        S.barrier()

    with ExitStack() as ph:
        wo1 = sbt(ph, "wo1", [128, 8, 1024], BF16)
        stg = sbt(ph, "stg3", [128, 8, 512], F32, 8)
        qz = sbt(ph, "qz", [128, 2, 4, 512], BF16)
        cgt = sbt(ph, "cgt3", [128, 4, 512], BF16)
        NKB = 3
        kth = [sbt(ph, f"kth{i}", [128, 512], BF16) for i in range(NKB)]
        vth = [sbt(ph, f"vth{i}", [128, 4, 129], BF16) for i in range(NKB)]
        den8 = sbt(ph, "den8", [128, 8], F32)
        r1n = sbt(ph, "r1n", [128, 4], F32)
        o_t = sbt(ph, "o_t", [128, 4, 128], F32)
        sq4 = sbt(ph, "sq4", [128, 4, 128], F32)
        ssq4 = sbt(ph, "ssq4", [128, 8], F32)
        ctok = sbt(ph, "ctok", [128, 4, 128], BF16)
        grow = sbt(ph, "grow", [128, 128], F32)
        accs = sbt(ph, "accs", [128, 8, 129], F32)
        Et = [sbt(ph, f"E3{i}", [128, 1024], BF16) for i in range(3)]
        fa = [sbt(ph, f"fa{i}", [128, 512], F32) for i in range(4)]
        osq = sbt(ph, "osq", [128, 512], BF16)
        CD = sbt(ph, "CD", [128, 8, 512], BF16, 8)
        x1c = sbt(ph, "x1c", [128, 8, 512], F32, 8)
        sqf = [sbt(ph, f"sqf{i}", [128, 512], BF16) for i in range(2)]
        rstd = sbt(ph, "rstd3", [128, 512], F32)
        class SS:
            pass
        sset3 = []
        for p_ in range(2):
            X = SS()
            X.xbk = sbt(ph, f"xbk3{p_}", [128, 768], BF16)
            X.dtk = sbt(ph, f"dtk3{p_}", [128, 32], F32)
            X.bct = sbt(ph, f"bct{p_}", [128, 4, 128], BF16)
            X.pv = [sbt(ph, f"pv3{p_}{i}", [128, 512], BF16) for i in range(2)]
            X.zsk = sbt(ph, f"zsk{p_}", [128, 512], BF16)
            X.G_sb = sbt(ph, f"G_sb{p_}", [128, 2, 128], F32)
            X.smc = [sbt(ph, f"smc{p_}{i}", [128, 24], F32) for i in range(2)]
            X.ahl = [sbt(ph, f"ahl{p_}{i}", [128, 2, 8], BF16) for i in range(2)]
            X.rhs = [[sbt(ph, f"rhs{p_}{d}{k}", [128, 8, 128], BF16) for k in range(2)] for d in range(2)]
            X.Mt = [sbt(ph, f"Mt{p_}{i}", [128, 8, 128], BF16) for i in range(2)]
            X.yo = [sbt(ph, f"yo{p_}{i}", [128, 512], F32) for i in range(2)]
            X.ysum = sbt(ph, f"ysum{p_}", [128, 512], F32)
            X.dtok = sbt(ph, f"dtok{p_}", [128, 512], BF16)
            sset3.append(X)
        Eh = [sbt(ph, f"Eh{i}", [128, 128], F32) for i in range(4)]
        ssq = sbt(ph, "ssq", [128, 2], F32)
        gssm = sbt(ph, "gssm", [128, 512], F32)
        pbA = [pst(ph, f"pd{i}", [128, 512], F32) for i in range(4)]
        SP = [pst(ph, f"sp{i}", [128, 1024], F32) for i in range(2)]
        ptb_ap = pbA[3].t[:].bitcast(BF16)

        load_weight(wo1, w_out1, 1024, stg)
        S.op("pool", lambda: nc.gpsimd.memset(qz.t[:], 0.0), [], qz.b)
        for i in range(NKB):
            S.op("pool", lambda: nc.gpsimd.memset(vth[i].t[:, :, 128:129], 1.0), [], vth[i].b)
        S.op("dve", lambda: nc.vector.tensor_scalar(out=grow.t[:], in0=rowp.t[:, 296:424], scalar1=1.0 - LAM_INIT, scalar2=None, op0=ALU.mult),
             rowp.b, grow.b)
        S.dma("sp", gssm.t[:], rowp_in[:, 512:1024].partition_broadcast(128), [], gssm.b, "gssm")
        lk = [0]
        for j, jb in enumerate(jobs if LV >= 4 else []):
            n, nq = jb["n"], jb["nq"]
            for t in range(nq):
                S.gbegin()
                S.dma("sp", qz.t[0:64, 0, :, :], QT[j][t, 0:64], [DB("QT", j, t)], qz.b, "p3q")
                S.dma("sp", qz.t[64:128, 1, :, :], QT[j][t, 64:128], [DB("QT", j, t)], qz.b, "p3q")
                S.dma("sp", cgt.t[:], CG[j][t], [DB("CG", j, t)], cgt.b, "p3q")
                S.gend()
                S.gbegin()
                for c2 in range(4):
                    S.dma("sp", x1c.t[:, 2 * c2:2 * c2 + 2, :], x1T[j][t, :, 2 * c2:2 * c2 + 2, :], [DB("x1T", j, t)], x1c.b[2 * c2:2 * c2 + 2], "p3x")
                S.gend()
                def ssd_loads(blk):
                    X = sset3[blk % 2]
                    c = t * 4 + blk
                    sn = f"p3s{blk % 2}"
                    S.gbegin()
                    S.dma("sp", X.xbk.t[:], XB[j][t, :, blk, :], [DB("XB", j, t)], X.xbk.b, sn)
                    S.dma("sp", X.dtk.t[:], DTA[j][t, :, blk, :], [DB("DTA", j, t)], X.dtk.b, sn)
                    S.dma("sp", X.bct.t[:], BCT[j][t, :, :, blk * 128:(blk + 1) * 128], [DB("BCT", j, t)], X.bct.b, sn)
                    for d in range(2):
                        S.dma("sp", X.pv[d].t[:], PRV[j][d, c], [DB("PRV", j, (d, c))], X.pv[d].b, sn)
                    S.dma("sp", X.zsk.t[:], ZS[j][t, :, blk, :], [DB("ZS", j, t)], X.zsk.b, sn)
                    S.gend()

                ssd_loads(0)
                S.tag = "datt"
                def acc_ap(r):
                    return pbA[r // 3].t[:, (r % 3) * 129:(r % 3) * 129 + 129]

                stream = [(hh_, kt_) for hh_ in range(4) for kt_ in range(n)]
                kbufs = {}

                def load_kv(i_):
                    hh_, kt_ = stream[i_]
                    i = lk[0] % NKB
                    lk[0] += 1
                    S.gbegin()
                    S.dma("sp", kth[i].t[:], KT[j][kt_, :, hh_, :], [DB("KT", j, kt_)], kth[i].b, f"p3k{i}")
                    S.dma("sp", vth[i].t[:, :, 0:128], VT[j][kt_, :, :, hh_ * 128:(hh_ + 1) * 128], [DB("VT", j, kt_)], vth[i].b, f"p3k{i}")
                    S.gend()
                    kbufs[(hh_, kt_)] = i

                load_kv(0)
                if len(stream) > 1:
                    load_kv(1)

                for hh in range(4):
                    steps2 = [(kt, kb) for kt in range(n) for kb in range(4)]

                    def emit_s(si):
                        kt, kb = steps2[si]
                        ki = kbufs[(hh, kt)]
                        sp = SP[si % 2]
                        for t2 in range(2):
                            mm(sp.t[:, t2 * 512:(t2 + 1) * 512], kth[ki].t[:, kb * 128:(kb + 1) * 128], qz.t[:, t2, hh, :], True, True,
                               kth[ki].b + qz.b, sp.b, t2 == 1)

                    emit_s(0)
                    for si in range(len(steps2)):
                        kt, kb = steps2[si]
                        if kb == 0 and hh * n + kt + 2 < len(stream):
                            load_kv(hh * n + kt + 2)
                        if si + 1 < len(steps2):
                            emit_s(si + 1)
                        sp = SP[si % 2]
                        E = Et[si % 3]
                        S.op("act", lambda: nc.scalar.activation(out=E.t[:], in_=sp.t[:], func=AF.Exp, scale=0.125), sp.b, E.b)
                        first = (kt == 0 and kb == 0)
                        last = (kt == n - 1 and kb == 3)
                        ki = kbufs[(hh, kt)]
                        for r in range(8):
                            t2, qb = r // 4, r % 4
                            mm(acc_ap(r), E.t[:, t2 * 512 + qb * 128:t2 * 512 + (qb + 1) * 128], vth[ki].t[:, kb, :],
                               first and (r % 3 == 0), last, vth[ki].b + E.b, pbA[r // 3].b, r == 7)
                    for b3 in range(3):
                        nr = 3 if b3 < 2 else 2
                        S.op("dve", lambda: nc.vector.tensor_copy(out=accs.t[:, 3 * b3:3 * b3 + nr, :].rearrange("p r c -> p (r c)"),
                                                                  in_=pbA[b3].t[:, 0:129 * nr]), pbA[b3].b, accs.b)
                    accb = accs.b
                    S.op("act", lambda: nc.scalar.activation(out=den8.t[:], in_=accs.t[:, :, 128], func=AF.Ln), accb, den8.b)
                    S.op("act", lambda: nc.scalar.activation(out=den8.t[:], in_=den8.t[:], func=AF.Exp, scale=-1.0), den8.b, den8.b)
                    S.op("dve", lambda: nc.vector.tensor_scalar(out=r1n.t[:], in0=den8.t[:, 4:8], scalar1=neglam.t[:, 0:1], scalar2=None, op0=ALU.mult),
                         den8.b + neglam.b, r1n.b)
                    for qb in range(4):
                        S.op("dve", lambda: nc.vector.tensor_scalar(out=o_t.t[:, qb, :], in0=accs.t[:, qb, 0:128], scalar1=den8.t[:, qb:qb + 1], scalar2=None,
                                                                    op0=ALU.mult), accb + den8.b, o_t.b)
                        S.op("dve", lambda: nc.vector.scalar_tensor_tensor(out=o_t.t[:, qb, :], in0=accs.t[:, 4 + qb, 0:128], scalar=r1n.t[:, qb:qb + 1],
                                                                           in1=o_t.t[:, qb, :], op0=ALU.mult, op1=ALU.add), accb + r1n.b + o_t.b, o_t.b)
                    S.op("act", lambda: nc.scalar.activation(out=sq4.t[:], in_=o_t.t[:], func=AF.Square), o_t.b, sq4.b)
                    S.op("dve", lambda: nc.vector.tensor_reduce(out=ssq4.t[:, 0:4], in_=sq4.t[:], axis=mybir.AxisListType.X, op=ALU.add), sq4.b, ssq4.b)
                    S.op("act", lambda: nc.scalar.activation(out=ssq4.t[:, 4:8], in_=ssq4.t[:, 0:4], func=AF.Ln, bias=epsc.t[:], scale=1.0 / 128),
                         ssq4.b + epsc.b, ssq4.b)
                    S.op("act", lambda: nc.scalar.activation(out=ssq4.t[:, 4:8], in_=ssq4.t[:, 4:8], func=AF.Exp, scale=-0.5), ssq4.b, ssq4.b)
                    S.op("dve", lambda: nc.vector.tensor_tensor(out=o_t.t[:], in0=o_t.t[:], in1=ssq4.t[:, 4:8].unsqueeze(2).to_broadcast([128, 4, 128]),
                                                                op=ALU.mult), o_t.b + ssq4.b, o_t.b)
                    S.op("dve", lambda: nc.vector.tensor_tensor(out=o_t.t[:], in0=o_t.t[:], in1=grow.t[:].unsqueeze(1).to_broadcast([128, 4, 128]),
                                                                op=ALU.mult), o_t.b + grow.b, o_t.b)
                    S.op("dve", lambda: nc.vector.tensor_tensor(out=ctok.t[:], in0=o_t.t[:], in1=cgt.t[:, :, hh * 128:(hh + 1) * 128], op=ALU.mult),
                         o_t.b + cgt.b, ctok.b)
                    for qb in range(4):
                        S.op("pe", lambda: nc.tensor.transpose(out=ptb_ap[:, qb * 128:(qb + 1) * 128], in_=ctok.t[:, qb, :], identity=identb.t[:]),
                             ctok.b + identb.b, pbA[3].b, inc=(qb == 3))
                    S.op("act", lambda: nc.scalar.copy(out=CD.t[:, hh, :], in_=ptb_ap[:, 0:512]), pbA[3].b, [CD.b[hh]])
                S.tag = "ssd"

                def ssd_front(blk):
                    X = sset3[blk % 2]
                    gps = pbA[0]
                    for g in range(2):
                        mm(gps.t[:, g * 128:(g + 1) * 128], X.bct.t[:, g, :], X.bct.t[:, 2 + g, :], True, True, X.bct.b, gps.b, g == 1)
                    S.op("act", lambda: nc.scalar.copy(out=X.G_sb.t[:].rearrange("p g l -> p (g l)"), in_=gps.t[:, 0:256]), gps.b, X.G_sb.b)
                    cps = pbA[2]
                    for d in range(2):
                        a_ap = X.dtk.t[:, 16 + 8 * d:24 + 8 * d]
                        mm(cps.t[:, 8 * d:8 * d + 8], triD[d], a_ap, True, True, cmat.b + X.dtk.b, cps.b, True)
                    for d in range(2):
                        a_ap = X.dtk.t[:, 16 + 8 * d:24 + 8 * d]
                        sc_ = X.smc[d]
                        S.op("dve", lambda: nc.vector.tensor_scalar(out=sc_.t[:, 0:8], in0=cps.t[:, 8 * d:8 * d + 8], scalar1=-1.0, scalar2=None,
                                                                    op0=ALU.mult), cps.b, sc_.b)
                        S.op("act", lambda: nc.scalar.activation(out=sc_.t[:, 8:16], in_=cps.t[:, 8 * d:8 * d + 8], func=AF.Exp), cps.b, sc_.b)
                        ahl = X.ahl[d]
                        S.op("dve", lambda: nc.vector.tensor_copy(out=ahl.t[:, 0, :], in_=a_ap), X.dtk.b, ahl.b)
                        S.op("dve", lambda: nc.vector.tensor_tensor(out=ahl.t[:, 1, :], in0=a_ap, in1=ahl.t[:, 0, :], op=ALU.subtract),
                             X.dtk.b + ahl.b, ahl.b)
                        for k in range(2):
                            S.op("dve", lambda: nc.vector.tensor_tensor(out=X.rhs[d][k].t[:], in0=triD[d].unsqueeze(1).to_broadcast([128, 8, 128]),
                                                                        in1=ahl.t[:, k, :].unsqueeze(2).to_broadcast([128, 8, 128]), op=ALU.mult),
                                 cmat.b + ahl.b, X.rhs[d][k].b)
                        ab = SP[d]
                        for half in range(2):
                            abh = ab.t[:, half * 512:(half + 1) * 512]
                            mm(abh, onesb.t[:], X.rhs[d][0].t[:, half * 4:(half + 1) * 4, :], True, False, onesb.b + X.rhs[d][0].b, ab.b, False)
                            mm(abh, onesb.t[:], X.rhs[d][1].t[:, half * 4:(half + 1) * 4, :], False, False, onesb.b + X.rhs[d][1].b, ab.b, False)
                            mm(abh, identb.t[:], mbias.t[:, d, :], False, True, identb.b + mbias.b, ab.b, True)

                def ssd_mid(blk):
                    X = sset3[blk % 2]
                    for d in range(2):
                        dt_ap = X.dtk.t[:, 8 * d:8 * d + 8]
                        sc_ = X.smc[d]
                        ab = SP[d]
                        for h in range(8):
                            e_ = Eh[h % 4]
                            S.op("act", lambda: nc.scalar.activation(out=e_.t[:], in_=ab.t[:, h * 128:(h + 1) * 128], func=AF.Exp,
                                                                     bias=sc_.t[:, h:h + 1], scale=1.0), ab.b + sc_.b, e_.b)
                            S.op("dve", lambda: nc.vector.scalar_tensor_tensor(out=X.Mt[d].t[:, h, :], in0=e_.t[:], scalar=dt_ap[:, h:h + 1],
                                                                               in1=X.G_sb.t[:, h // 4, :], op0=ALU.mult, op1=ALU.mult),
                                 e_.b + X.dtk.b + X.G_sb.b, X.Mt[d].b)

                def ssd_back(blk):
                    X = sset3[blk % 2]
                    yps = pbA[1]
                    for d in range(2):
                        sc_ = X.smc[d]
                        for h in range(8):
                            mm(yps.t[:, h * 64:(h + 1) * 64], X.Mt[d].t[:, h, :], X.xbk.t[:, h * 64:(h + 1) * 64], d == 0 and h == 0, False,
                               X.Mt[d].b + X.xbk.b, yps.b, False)
                        ops_ = pbA[0]
                        for g in range(2):
                            mm(ops_.t[:, g * 256:(g + 1) * 256], X.bct.t[:, 2 + g, :], X.pv[d].t[:, g * 256:(g + 1) * 256], True, True,
                               X.bct.b + X.pv[d].b, ops_.b, g == 1)
                        S.op("dve", lambda: nc.vector.tensor_tensor(out=X.yo[d].t[:].rearrange("p (h d) -> p h d", h=8),
                                                                    in0=ops_.t[:].rearrange("p (h d) -> p h d", h=8),
                                                                    in1=sc_.t[:, 8:16].unsqueeze(2).to_broadcast([128, 8, 64]), op=ALU.mult),
                             ops_.b + sc_.b, X.yo[d].b)
                    for h in range(8):
                        mm(yps.t[:, h * 64:(h + 1) * 64], identD.t[:, h, :], X.xbk.t[:, h * 64:(h + 1) * 64], False, True, identD.b + X.xbk.b, yps.b, h == 7)
                    ysum, yo, dtok = X.ysum, X.yo, X.dtok
                    S.op("dve", lambda: nc.vector.tensor_tensor(out=ysum.t[:], in0=yps.t[:], in1=yo[0].t[:], op=ALU.add), yps.b + yo[0].b, ysum.b)
                    S.op("dve", lambda: nc.vector.tensor_tensor(out=ysum.t[:], in0=ysum.t[:], in1=yo[1].t[:], op=ALU.add), ysum.b + yo[1].b, ysum.b)
                    S.op("dve", lambda: nc.vector.tensor_tensor(out=ysum.t[:], in0=ysum.t[:], in1=X.zsk.t[:], op=ALU.mult), ysum.b + X.zsk.b, ysum.b)
                    S.op("act", lambda: nc.scalar.activation(out=yo[0].t[:], in_=ysum.t[:], func=AF.Square), ysum.b, yo[0].b)
                    S.op("dve", lambda: nc.vector.tensor_reduce(out=ssq.t[:, 0:1], in_=yo[0].t[:], axis=mybir.AxisListType.X, op=ALU.add),
                         yo[0].b, ssq.b)
                    S.op("act", lambda: nc.scalar.activation(out=ssq.t[:, 1:2], in_=ssq.t[:, 0:1], func=AF.Ln, bias=epsc.t[:], scale=1.0 / 512),
                         ssq.b + epsc.b, ssq.b)
                    S.op("act", lambda: nc.scalar.activation(out=ssq.t[:, 1:2], in_=ssq.t[:, 1:2], func=AF.Exp, scale=-0.5), ssq.b, ssq.b)
                    S.op("dve", lambda: nc.vector.scalar_tensor_tensor(out=dtok.t[:], in0=ysum.t[:], scalar=ssq.t[:, 1:2], in1=gssm.t[:],
                                                                       op0=ALU.mult, op1=ALU.mult), ysum.b + ssq.b + gssm.b, dtok.b)
                    for c4 in range(4):
                        S.op("pe", lambda: nc.tensor.transpose(out=ptb_ap[:, c4 * 128:(c4 + 1) * 128], in_=dtok.t[:, c4 * 128:(c4 + 1) * 128],
                                                               identity=identb.t[:]), dtok.b + identb.b, pbA[3].b, inc=(c4 == 3))
                    S.op("act", lambda: nc.scalar.copy(out=CD.t[:, 4:8, blk * 128:(blk + 1) * 128],
                                                       in_=ptb_ap[:, 0:512].rearrange("p (c l) -> p c l", c=4)), pbA[3].b, CD.b[4:8])

                ssd_front(0)
                for blk in range(4):
                    ssd_mid(blk)
                    if blk + 1 < 4:
                        ssd_loads(blk + 1)
                        ssd_front(blk + 1)
                    ssd_back(blk)
                S.tag = "fin"
                if dbg:
                    S.dma("pool", CDd[j][t], CD.t[:], CD.b, [DB("CDd", j, t)], "wcd")
                for oc in range(8):
                    yps = pbA[oc % 2]
                    for k in range(8):
                        mm(yps.t[:], wo1.t[:, k, oc * 128:(oc + 1) * 128], CD.t[:, k, :], k == 0, k == 7, wo1.b + [CD.b[k]], yps.b, k == 7)
                    S.op("dve", lambda: nc.vector.tensor_tensor(out=x1c.t[:, oc, :], in0=yps.t[:], in1=x1c.t[:, oc, :], op=ALU.add),
                         yps.b + [x1c.b[oc]], [x1c.b[oc]])
                for c in range(8):
                    s = sqf[c % 2]
                    S.op("act", lambda: nc.scalar.activation(out=s.t[:], in_=x1c.t[:, c, :], func=AF.Square), [x1c.b[c]], s.b)
                    mm(pbA[2].t[:], onesb.t[:], s.t[:], c == 0, c == 7, s.b + onesb.b, pbA[2].b, True)
                S.op("act", lambda: nc.scalar.activation(out=rstd.t[:], in_=pbA[2].t[:], func=AF.Ln, bias=epsc.t[:], scale=1.0 / 1024),
                     pbA[2].b + epsc.b, rstd.b)
                S.op("act", lambda: nc.scalar.activation(out=rstd.t[:], in_=rstd.t[:], func=AF.Exp, scale=-0.5), rstd.b, rstd.b)
                S.gbegin()
                for c in range(8):
                    S.op("dve", lambda: nc.vector.scalar_tensor_tensor(out=x1c.t[:, c, :], in0=x1c.t[:, c, :], scalar=colp.t[:, 16 + c:17 + c],
                                                                       in1=rstd.t[:], op0=ALU.mult, op1=ALU.mult),
                         [x1c.b[c]] + rstd.b + colp.b, [x1c.b[c]])
                    if c % 2 == 1:
                        S.dma("pool", y_out[j][t, :, c - 1:c + 1, :], x1c.t[:, c - 1:c + 1, :], x1c.b[c - 1:c + 1], [DB("yout", j, t)], "yw")
                S.gend()
        S.barrier()
    es.close()
    print("instr counts", S.check_deadlock(), "max sem", max(S.cnt.values()), "nsem", len(S.cnt))
    return nc


QPERM = [0, 4, 1, 5, 2, 6, 3, 7]


def _consts():
    cmat = np.zeros((128, 9, 128), np.float32)
    cmat[:, 0, :] = np.eye(128)
    cmat[:, 1, :] = 1.0
    R = np.zeros((128, 128), np.float32)
    for f2 in range(128):
        if f2 % 64 < 32:
            R[f2 + 32, f2] = -1.0
        else:
            R[f2 - 32, f2] = 1.0
    cmat[:, 2, :] = R
    lp = np.arange(128)[:, None]
    l = np.arange(128)[None, :]
    cmat[:, 3, :] = (lp <= l)
    cmat[:, 4, :] = (lp >= l)
    cmask = np.zeros((128, 4, 512), np.float32)
    kk = np.arange(128)[:, None]
    qq = np.arange(128)[None, :]
    cmask[:, 0, :] = np.tile((qq <= kk).astype(np.float32), (1, 4))
    cmask[:, 1, :] = np.tile((kk <= qq).astype(np.float32), (1, 4))
    s = kk
    cmask[:, 2, :] = np.tile(np.where(qq >= s, 0.0, -30000.0).astype(np.float32), (1, 4))
    cmask[:, 3, :] = np.tile(np.where(qq <= s, 0.0, -30000.0).astype(np.float32), (1, 4))
    return cmat, cmask


def _cols(v, nch):
    return np.ascontiguousarray(np.asarray(v, np.float32).reshape(nch, 128).T)


def prep_weights(p):
    d = {}
    w_in0 = np.asarray(p["w_in0"][0], np.float32)
    hp = np.concatenate([np.arange(h * 64, (h + 1) * 64) for h in QPERM])
    cols = np.arange(2816)
    cols[1536:2048] = 1536 + hp
    cols[2304:2816] = 2304 + hp
    d["w_in0"] = np.ascontiguousarray(w_in0[:, cols])
    w_out0 = np.asarray(p["w_out0"][0], np.float32)
    rows = np.arange(1024)
    rows[512:1024] = 512 + hp
    d["w_out0"] = np.ascontiguousarray(w_out0[rows, :])
    d["w_in1"] = np.ascontiguousarray(np.asarray(p["w_in1"][0], np.float32))
    d["w_out1"] = np.ascontiguousarray(np.asarray(p["w_out1"][0], np.float32))
    colp = np.zeros((128, 64), np.float32)
    colp[:, 0:8] = _cols(p["norm_g"][0], 8)
    colp[:, 8:16] = _cols(p["norm_g"][1], 8)
    colp[:, 16:24] = _cols(p["final_norm_g"], 8)
    colp[:, 24:28] = _cols(p["conv_b"][0], 4)
    colp[:, 28:32] = _cols(p["conv_ln_g"][0], 4)
    colp[:, 32:36] = _cols(p["conv_ln_b"][0], 4)
    sink = np.asarray(p["sink"][0], np.float32)
    colp[64:128, 36:40] = sink[0:4][None, :]
    colp[0:64, 36:40] = sink[4:8][None, :]
    colp[:, 40:48] = _cols(p["ssm_conv_b"][0], 8)
    colp[:, 48] = np.asarray(p["diff_norm_g"][0], np.float32)
    d["colp"] = colp
    d["cw"] = np.ascontiguousarray(np.asarray(p["conv_w"][0], np.float32).reshape(31, 4, 128).transpose(2, 1, 0))
    d["scw"] = np.ascontiguousarray(np.asarray(p["ssm_conv_w"][0], np.float32).reshape(5, 8, 128).transpose(2, 1, 0))
    rowp = np.zeros((1, 1024), np.float32)
    rowp[0, 0:64] = p["lambda_q1"][0]
    rowp[0, 64:128] = p["lambda_k1"][0]
    rowp[0, 128:192] = p["lambda_q2"][0]
    rowp[0, 192:256] = p["lambda_k2"][0]
    rowp[0, 256:264] = p["dt_bias_f"][0]
    rowp[0, 264:272] = p["dt_bias_b"][0]
    rowp[0, 272:280] = p["a_log_f"][0]
    rowp[0, 280:288] = p["a_log_b"][0]
    rowp[0, 288:296] = p["d_skip"][0]
    rowp[0, 296:424] = p["diff_norm_g"][0]
    rowp[0, 512:1024] = p["ssm_norm_g"][0]
    d["rowp"] = rowp
    d["cmat"], d["cmask"] = _consts()
    return d


def prep_job(x, q0, nq):
    Sx = x.shape[0]
    n = Sx // T
    xpad = np.zeros((Sx + 2 * H, 1024), np.float32)
    xpad[H:H + Sx] = x
    inv = (1.0 / (10000.0 ** (np.arange(0, 64, 2, dtype=np.float32) / 64))).astype(np.float32)
    xt = np.empty((n, 128, 8, TW), np.float32)
    cs = np.empty((n, 128, 2, TW), np.float32)
    kv = np.ones((n, 128, 2), np.float32)
    hf = np.ones((n, 128, 2), np.float32)
    kf = np.ones((128, 2, n * 4), np.float32)
    for i in range(n):
        tt = (q0 + i) % n
        seg = xpad[tt * T:tt * T + TW]
        xt[i] = seg.T.reshape(8, 128, TW).transpose(1, 0, 2)
        pos = (np.arange(TW, dtype=np.float32) + np.float32(tt * T - H))
        f = pos[None, :] * inv[:, None]
        emb = np.concatenate([f, f, f, f], axis=0)
        cs[i, :, 0, :] = np.cos(emb)
        cs[i, :, 1, :] = np.sin(emb)
        if tt == 0:
            kv[i, :, 0] = 0.0
            hf[i, :, 0] = 0.0
            kf[:, 0, i * 4] = 0.0
        if tt == n - 1:
            kv[i, :, 1] = 0.0
            hf[i, :, 1] = 0.0
            kf[:, 1, i * 4 + 3] = 0.0
    return xt, cs, kv, hf, kf


JOBS = [dict(n=8, nq=8), dict(n=8, nq=8), dict(n=16, nq=4)]
_NC_CACHE = {}


def kernel(x_prompt, x_sample, **p):
    x_prompt = np.asarray(x_prompt, np.float32)
    x_sample = np.asarray(x_sample, np.float32)
    wd = prep_weights(p)
    key = "main"
    if key not in _NC_CACHE:
        _NC_CACHE[key] = build(JOBS)
    nc = _NC_CACHE[key]
    in_maps = []
    for c in range(8):
        m = dict(wd)
        specs = [(x_sample[2 * c], 0, 8), (x_sample[2 * c + 1], 0, 8), (x_prompt[c // 4], (c % 4) * 4, 4)]
        for j, (xs, q0, nq) in enumerate(specs):
            xt, cs, kv, hf, kf = prep_job(xs, q0, nq)
            m[f"xt{j}"], m[f"cs{j}"], m[f"kv{j}"], m[f"hf{j}"], m[f"kf{j}"] = xt, cs, kv, hf, kf
        in_maps.append(m)
    res = run_bass_kernel_spmd(nc, in_maps, core_ids=list(range(8)))
    y_prompt = np.empty((2, 8192, 1024), np.float32)
    y_sample = np.empty((16, 4096, 1024), np.float32)

    def untile(yt):
        nq = yt.shape[0]
        return yt.transpose(0, 3, 2, 1).reshape(nq * T, 1024)

    for c in range(8):
        r = res.results[c]
        y_sample[2 * c] = untile(r["yt0"])
        y_sample[2 * c + 1] = untile(r["yt1"])
        q = c % 4
        y_prompt[c // 4, q * 2048:(q + 1) * 2048] = untile(r["yt2"])
    return (y_prompt, y_sample)
```

```python
import math
from contextlib import ExitStack
import numpy as np
import concourse.bass as bass
import concourse.mybir as mybir
from concourse.bass_utils import run_bass_kernel_spmd

F32 = mybir.dt.float32
BF16 = mybir.dt.bfloat16
AF = mybir.ActivationFunctionType
ALU = mybir.AluOpType

T = 512
H = 128
TW = T + 2 * H
EPS = 1e-6
LAM_INIT = 0.8 - 0.6 * math.exp(-0.3 * 1)
STRICT_SAME_ENGINE = False


class Buf:
    __slots__ = ("w", "r")

    def __init__(self):
        self.w = None
        self.r = []


class Tl:
    def __init__(self, t, nsub=1):
        self.t = t
        self.b = [Buf() for _ in range(nsub)]


class Sched:
    def __init__(self, nc, es):
        self.nc = nc
        self.es = es
        self.eng = {"pe": nc.tensor, "act": nc.scalar, "dve": nc.vector, "pool": nc.gpsimd, "sp": nc.sync}
        self.semobj = {}
        self.cnt = {}
        self.cur = {}
        self.gen = 0
        self.new_engine_sems()
        self.known = {e: {} for e in self.eng}
        self.pending_pe = False
        self.grp = None
        self.tag = None
        self.prog = {e: [] for e in self.eng}
        self.pend = {e: [] for e in self.eng}

    def new_engine_sems(self):
        self.gen += 1
        for e in ["pe", "act", "dve", "pool"]:
            key = f"{e}.{self.gen}"
            self.semobj[key] = self.es.enter_context(self.nc.semaphore("s_" + e + str(self.gen)))
            self.cnt[key] = 0
            self.cur[e] = key

    def _need(self, e, reads, writes):
        need = {}

        def add(tok, kind):
            if tok is None:
                return
            k, v = tok
            if k == self.cur.get(e):
                if e == "pe":
                    return
                if kind != "raw" and not STRICT_SAME_ENGINE:
                    return
            if need.get(k, 0) < v:
                need[k] = v

        for b in reads:
            add(b.w, "raw")
        for b in writes:
            add(b.w, "waw")
            for tk in b.r:
                add(tk, "war")
        return need

    def _emit_waits(self, e, need):
        kn = self.known[e]
        for k, v in need.items():
            if kn.get(k, 0) >= v:
                continue
            self.eng[e].wait_ge(self.semobj[k], v)
            self.pend[e].append((k, v))
            kn[k] = v

    def _update(self, tok, reads, writes):
        for b in reads:
            b.r.append(tok)
            if len(b.r) > 24:
                mx = {}
                for k, v in b.r:
                    if mx.get(k, 0) < v:
                        mx[k] = v
                b.r = list(mx.items())
        for b in writes:
            b.w = tok
            b.r = []

    def op(self, e, fn, reads=(), writes=(), inc=True):
        need = self._need(e, reads, writes)
        self._emit_waits(e, need)
        inst = fn()
        if self.tag is not None:
            inst.annotate(self.tag)
        ck = self.cur[e]
        self.prog[e].append((self.pend[e], ck if inc else None, 1))
        self.pend[e] = []
        if inc:
            self.cnt[ck] += 1
            inst.then_inc(self.semobj[ck], 1)
            tok = (ck, self.cnt[ck])
            if e == "pe":
                self.pending_pe = False
        else:
            assert e == "pe"
            tok = (ck, self.cnt[ck] + 1)
            self.pending_pe = True
        self._update(tok, reads, writes)
        return inst

    def dma(self, q, out, in_, reads, writes, sname):
        if sname not in self.semobj:
            self.semobj[sname] = self.es.enter_context(self.nc.semaphore("d_" + sname))
            self.cnt[sname] = 0
        need = self._need(q, reads, writes)
        self._emit_waits(q, need)
        self.eng[q].dma_start(out=out, in_=in_).then_inc(self.semobj[sname], 16)
        self.prog[q].append((self.pend[q], sname, 16))
        self.pend[q] = []
        self.cnt[sname] += 16
        tok = (sname, self.cnt[sname])
        self._update(tok, reads, writes)
        if self.grp is not None:
            self.grp.append((tok, list(reads), list(writes)))

    def gbegin(self):
        self.grp = []

    def gend(self):
        g, self.grp = self.grp, None
        final = {}
        for (k, v), _, _ in g:
            final[k] = max(final.get(k, 0), v)
        for (k, v), reads, writes in g:
            ft = (k, final[k])
            for b in writes:
                if b.w is not None and b.w[0] == k:
                    b.w = ft
            for b in reads:
                b.r = [ft if tk[0] == k else tk for tk in b.r]

    def check_deadlock(self):
        prog = {e: list(p) for e, p in self.prog.items()}
        for e in prog:
            if self.pend[e]:
                prog[e].append((self.pend[e], None, 0))
        val = {}
        ptr = {e: 0 for e in prog}
        progress = True
        while progress:
            progress = False
            for e, p in prog.items():
                while ptr[e] < len(p):
                    waits, k, amt = p[ptr[e]]
                    if all(val.get(wk, 0) >= wv for wk, wv in waits):
                        if k is not None:
                            val[k] = val.get(k, 0) + amt
                        ptr[e] += 1
                        progress = True
                    else:
                        break
        stuck = {e: (ptr[e], len(p), p[ptr[e]][0]) for e, p in prog.items() if ptr[e] < len(p)}
        if stuck:
            raise RuntimeError(f"DEADLOCK: {stuck} vals={ {k: v for k, v in val.items()} }")
        return {e: len(p) for e, p in prog.items()}

    def barrier(self):
        assert not self.pending_pe
        for e in self.eng:
            need = {k: v for k, v in self.cnt.items() if v > 0 and k != self.cur.get(e)}
            self._emit_waits(e, need)
        self.new_engine_sems()


class StopPhase(Exception):
    pass


def build(jobs, dbg=False, LV=9, SUB=99):
    def ck(k):
        if SUB == k:
            raise StopPhase()
    nc = bass.Bass("TRN2", target_bir_lowering=False)
    NJ = len(jobs)

    def din(name, shape, dt=F32):
        return nc.dram_tensor(name, list(shape), dt, kind="ExternalInput").ap()

    def dscr(name, shape, dt):
        return nc.dram_tensor(name, list(shape), dt, kind="ExternalOutput" if dbg else "Internal").ap()

    xt_in, cs_in, kv_in, hf_in, kf_in, y_out = [], [], [], [], [], []
    x1T, QT, KT, VT, CG, ZS, BCT, XB, DTA, PRV = [], [], [], [], [], [], [], [], [], []
    CDd = []
    for j, jb in enumerate(jobs):
        n, nq = jb["n"], jb["nq"]
        xt_in.append(din(f"xt{j}", [n, 128, 8, TW]))
        cs_in.append(din(f"cs{j}", [n, 128, 2, TW]))
        kv_in.append(din(f"kv{j}", [n, 128, 2]))
        hf_in.append(din(f"hf{j}", [n, 128, 2]))
        kf_in.append(din(f"kf{j}", [128, 2, n * 4]))
        y_out.append(nc.dram_tensor(f"yt{j}", [nq, 128, 8, T], F32, kind="ExternalOutput").ap())
        x1T.append(dscr(f"x1T{j}", [n, 128, 8, T], F32))
        QT.append(dscr(f"QT{j}", [nq, 128, 4, T], BF16))
        KT.append(dscr(f"KT{j}", [n, 128, 4, T], BF16))
        VT.append(dscr(f"VT{j}", [n, 128, 4, 512], BF16))
        CG.append(dscr(f"CG{j}", [nq, 128, 4, T], BF16))
        ZS.append(dscr(f"ZS{j}", [nq, 128, 4, 512], BF16))
        BCT.append(dscr(f"BCT{j}", [nq, 128, 4, T], BF16))
        XB.append(dscr(f"XB{j}", [n, 128, 4, 768], BF16))
        DTA.append(dscr(f"DTA{j}", [n, 128, 4, 32], F32))
        PRV.append(dscr(f"PRV{j}", [2, n * 4, 128, 512], BF16))
        if dbg:
            CDd.append(dscr(f"CDd{j}", [nq, 128, 8, T], BF16))
    w_in0 = din("w_in0", [1024, 2816])
    w_out0 = din("w_out0", [1024, 1024])
    w_in1 = din("w_in1", [1024, 3600])
    w_out1 = din("w_out1", [1024, 1024])
    colp_in = din("colp", [128, 64])
    cw_in = din("cw", [128, 4, 31])
    scw_in = din("scw", [128, 8, 5])
    rowp_in = din("rowp", [1, 1024])
    cmat_in = din("cmat", [128, 9, 128])
    cmask_in = din("cmask", [128, 4, 512])

    es = ExitStack()
    S = Sched(nc, es)

    def sbt(st, name, shape, dt, nsub=1):
        return Tl(st.enter_context(nc.sbuf_tensor("s_" + name, list(shape), dt)), nsub)

    def pst(st, name, shape, dt):
        return Tl(st.enter_context(nc.psum_tensor("p_" + name, list(shape), dt)), 1)

    dbuf = {}

    def DB(name, j, i):
        key = (name, j, i)
        if key not in dbuf:
            dbuf[key] = Buf()
        return dbuf[key]

    colp = sbt(es, "colp", [128, 64], F32)
    cw = sbt(es, "cw", [128, 4, 31], F32)
    scw = sbt(es, "scw", [128, 8, 5], F32)
    rowp = sbt(es, "rowp", [128, 512], F32)
    cmat = sbt(es, "cmat", [128, 9, 128], F32)
    onec = sbt(es, "onec", [128, 1], F32)
    identb = sbt(es, "identb", [128, 128], BF16)
    onesb = sbt(es, "onesb", [128, 128], BF16)
    maskLR = sbt(es, "maskLR", [128, 2, 512], BF16)
    mbias = sbt(es, "mbias", [128, 2, 512], BF16)
    epsc = sbt(es, "epsc", [128, 1], F32)
    esink = sbt(es, "esink", [128, 4], F32)
    abc = sbt(es, "abc", [128, 16], F32)
    neglam = sbt(es, "neglam", [128, 1], F32)
    lamt = sbt(es, "lamt", [128, 4], F32)
    gdiff = sbt(es, "gdiff", [128, 1], F32)
    identD = sbt(es, "identD", [128, 8, 128], BF16)
    junk = sbt(es, "junk", [128, 64], F32)
    es_tmp = ExitStack()
    cmask = sbt(es_tmp, "cmask", [128, 4, 512], F32)

    S.gbegin()
    S.dma("sp", colp.t[:], colp_in, [], colp.b, "c0")
    S.dma("sp", cw.t[:], cw_in, [], cw.b, "c0")
    S.dma("sp", scw.t[:], scw_in, [], scw.b, "c0")
    S.dma("sp", rowp.t[:], rowp_in[:, 0:512].partition_broadcast(128), [], rowp.b, "c0")
    S.dma("sp", cmat.t[:], cmat_in, [], cmat.b, "c0")
    S.dma("sp", cmask.t[:], cmask_in, [], cmask.b, "c0")
    S.gend()
    S.op("dve", lambda: nc.vector.memset(onec.t[:], 1.0), [], onec.b)
    identf = cmat.t[:, 0, :]
    onesf = cmat.t[:, 1, :]
    Rf = cmat.t[:, 2, :]
    triD = [cmat.t[:, 3, :], cmat.t[:, 4, :]]
    S.op("dve", lambda: nc.vector.tensor_copy(out=identb.t[:], in_=identf), cmat.b, identb.b)
    S.op("dve", lambda: nc.vector.tensor_copy(out=onesb.t[:], in_=onesf), cmat.b, onesb.b)
    S.op("dve", lambda: nc.vector.tensor_copy(out=maskLR.t[:], in_=cmask.t[:, 0:2, :]), cmask.b, maskLR.b)
    S.op("dve", lambda: nc.vector.tensor_copy(out=mbias.t[:], in_=cmask.t[:, 2:4, :]), cmask.b, mbias.b)
    S.op("dve", lambda: nc.vector.memset(epsc.t[:], EPS), [], epsc.b)
    S.op("act", lambda: nc.scalar.activation(out=esink.t[:], in_=colp.t[:, 36:40], func=AF.Exp), colp.b, esink.b)
    S.op("act", lambda: nc.scalar.activation(out=abc.t[:], in_=rowp.t[:, 272:288], func=AF.Exp), rowp.b, abc.b)
    S.op("dve", lambda: nc.vector.tensor_scalar(out=abc.t[:], in0=abc.t[:], scalar1=-1.0, scalar2=None, op0=ALU.mult),
         abc.b, abc.b)
    S.op("dve", lambda: nc.vector.tensor_tensor(out=junk.t[:], in0=rowp.t[:, 0:64], in1=rowp.t[:, 64:128], op=ALU.mult),
         rowp.b, junk.b)
    S.op("dve", lambda: nc.vector.tensor_reduce(out=lamt.t[:, 0:1], in_=junk.t[:], axis=mybir.AxisListType.X, op=ALU.add),
         junk.b, lamt.b)
    S.op("dve", lambda: nc.vector.tensor_tensor(out=junk.t[:], in0=rowp.t[:, 128:192], in1=rowp.t[:, 192:256], op=ALU.mult),
         rowp.b + lamt.b, junk.b)
    S.op("dve", lambda: nc.vector.tensor_reduce(out=lamt.t[:, 1:2], in_=junk.t[:], axis=mybir.AxisListType.X, op=ALU.add),
         junk.b, lamt.b)
    S.op("act", lambda: nc.scalar.activation(out=lamt.t[:, 2:4], in_=lamt.t[:, 0:2], func=AF.Exp), lamt.b, lamt.b)
    S.op("dve", lambda: nc.vector.tensor_tensor(out=neglam.t[:], in0=lamt.t[:, 3:4], in1=lamt.t[:, 2:3], op=ALU.subtract),
         lamt.b, neglam.b)
    S.op("dve", lambda: nc.vector.tensor_scalar(out=neglam.t[:], in0=neglam.t[:], scalar1=-LAM_INIT, scalar2=None,
                                                op0=ALU.add), neglam.b, neglam.b)
    S.op("dve", lambda: nc.vector.tensor_scalar(out=gdiff.t[:], in0=colp.t[:, 48:49], scalar1=1.0 - LAM_INIT, scalar2=None,
                                                op0=ALU.mult), colp.b, gdiff.b)
    for h in range(8):
        S.op("dve", lambda h=h: nc.vector.tensor_scalar(out=identD.t[:, h, :], in0=identf, scalar1=rowp.t[:, 288 + h:289 + h],
                                                        scalar2=None, op0=ALU.mult), cmat.b + rowp.b, identD.b)
    S.barrier()
    es_tmp.close()

    def mm(out_ap, lhsT, rhs, start, stop, reads, writes, inc):
        S.op("pe", lambda: nc.tensor.matmul(out_ap, lhsT=lhsT, rhs=rhs, start=start, stop=stop), reads, writes, inc=inc)

    def load_weight(dst, src, ncols, stage, k_chunks=8):
        for k in range(k_chunks):
            sl = k % 2
            st_ap = stage.t[:, sl * 4:(sl + 1) * 4, :].rearrange("p a b -> p (a b)")[:, 0:ncols]
            sb = stage.b[sl * 4:(sl + 1) * 4]
            half = ncols // 2
            S.gbegin()
            S.dma("sp", st_ap[:, 0:half], src[k * 128:(k + 1) * 128, 0:half], [], sb, f"wst{sl}")
            S.dma("sp", st_ap[:, half:ncols], src[k * 128:(k + 1) * 128, half:ncols], [], sb, f"wst{sl}")
            S.gend()
            if k % 2 == 0:
                S.op("act", lambda: nc.scalar.copy(out=dst.t[:, k, :], in_=st_ap), sb, dst.b)
            else:
                S.op("dve", lambda: nc.vector.tensor_copy(out=dst.t[:, k, :], in_=st_ap), sb, dst.b)

    def rmsnorm_fm(xsrc, xb, width, ranges, gcol0, hT, sq, psA, psB, rstd):
        pss = [psA, psB]
        for c in range(8):
            s = sq[c % 2]
            S.op("act", lambda: nc.scalar.activation(out=s.t[:, 0:width], in_=xsrc(c), func=AF.Square), [xb[c]], s.b)
            for i, (a, b_) in enumerate(ranges):
                mm(pss[i].t[:, 0:b_ - a], onesb.t[:], s.t[:, a:b_], c == 0, c == 7, s.b + onesb.b, pss[i].b, i == len(ranges) - 1)
        for i, (a, b_) in enumerate(ranges):
            S.op("act", lambda: nc.scalar.activation(out=rstd.t[:, a:b_], in_=pss[i].t[:, 0:b_ - a], func=AF.Ln,
                                                     bias=epsc.t[:], scale=1.0 / 1024), pss[i].b + epsc.b, rstd.b)
        S.op("act", lambda: nc.scalar.activation(out=rstd.t[:, 0:width], in_=rstd.t[:, 0:width], func=AF.Exp, scale=-0.5), rstd.b, rstd.b)
        for c in range(8):
            S.op("dve", lambda: nc.vector.scalar_tensor_tensor(out=hT.t[:, c, :], in0=xsrc(c), scalar=colp.t[:, gcol0 + c:gcol0 + c + 1],
                                                               in1=rstd.t[:, 0:width], op0=ALU.mult, op1=ALU.mult),
                 [xb[c]] + rstd.b + colp.b, hT.b)

    def rope(ps, width, cos_ap, sin_ap, cs_b, qf, rot_ps, tmp, out_ap, out_b, outs=None):
        S.op("act", lambda: nc.scalar.copy(out=qf.t[:, 0:width], in_=ps.t[:, 0:width]), ps.b, qf.b)
        mm(rot_ps.t[:, 0:width], Rf, qf.t[:, 0:width], True, True, qf.b + cmat.b, rot_ps.b, True)
        S.op("dve", lambda: nc.vector.tensor_tensor(out=tmp.t[:, 0:width], in0=rot_ps.t[:, 0:width], in1=sin_ap, op=ALU.mult),
             rot_ps.b + cs_b, tmp.b)
        S.op("dve", lambda: nc.vector.tensor_tensor(out=qf.t[:, 0:width], in0=qf.t[:, 0:width], in1=cos_ap, op=ALU.mult),
             qf.b + cs_b, qf.b)
        if outs is None:
            S.op("dve", lambda: nc.vector.tensor_tensor(out=out_ap, in0=qf.t[:, 0:width], in1=tmp.t[:, 0:width], op=ALU.add),
                 qf.b + tmp.b, out_b)
        else:
            for psl, oap in outs:
                S.op("dve", lambda: nc.vector.tensor_tensor(out=oap, in0=qf.t[psl, 0:width], in1=tmp.t[psl, 0:width], op=ALU.add),
                     qf.b + tmp.b, out_b)

    with ExitStack() as ph:
        w0 = sbt(ph, "w0", [128, 8, 2816], BF16)
        wo = sbt(ph, "wo", [128, 8, 1024], BF16)
        diag = sbt(ph, "diag", [128, 124, 128], BF16)
        xt = sbt(ph, "xt", [128, 8, TW], F32, 8)
        cs = sbt(ph, "cs", [128, 2, TW], F32)
        kvf2 = [sbt(ph, f"kvf{i}", [128, 2], F32) for i in range(2)]
        sq = [sbt(ph, f"sq{i}", [128, TW], BF16) for i in range(2)]
        rstd = sbt(ph, "rstd", [128, TW], F32)
        hT = sbt(ph, "hT", [128, 8, TW], BF16)
        a_sb = sbt(ph, "a_sb", [128, 4, 544], BF16, 4)
        sg = [sbt(ph, f"sg{i}", [128, 512], BF16) for i in range(2)]
        co = sbt(ph, "co", [128, 4, 512], F32, 4)
        cobf0 = sbt(ph, "cobf0", [128, 512], BF16)
        cobf = [cobf0, cobf0]
        co20 = sbt(ph, "co20", [128, 512], BF16)
        co2 = [co20, co20]
        mean_sb = sbt(ph, "mean_sb", [128, 512], F32)
        rs2 = sbt(ph, "rs2", [128, 512], F32)
        qf0 = sbt(ph, "qf0", [128, 512], F32)
        qf = [qf0, qf0]
        tmpf0 = sbt(ph, "tmpf0", [128, 512], F32)
        tmpf = [tmpf0, tmpf0]
        qr = sbt(ph, "qr", [128, 4, 512], BF16, 4)
        kr = sbt(ph, "kr", [128, 2, TW], BF16)
        vaug = sbt(ph, "vaug", [128, 6, 2, 128], BF16)
        sbg = sbt(ph, "sbg", [128, 4, 512], BF16, 4)
        Et = [sbt(ph, f"Et{i}", [128, 512], BF16) for i in range(3)]
        rd = tmpf0
        AO = sbt(ph, "AO", [128, 8, 512], BF16, 8)
        xc0 = sbt(ph, "xc0", [128, 2, 512], F32)
        xc2 = [xc0, xc0]
        pb = [pst(ph, f"pb{i}", [128, 512], F32) for i in range(8)]

        load_weight(w0, w_in0, 2816, xt)
        load_weight(wo, w_out0, 1024, xt)
        S.op("pool", lambda: nc.gpsimd.memset(vaug.t[:], 1.0), [], vaug.b)
        S.op("pool", lambda: nc.gpsimd.memset(kr.t[:], 0.0), [], kr.b)
        for cc in range(4):
            for jj in range(31):
                S.op("dve", lambda: nc.vector.tensor_scalar(out=diag.t[:, cc * 31 + jj, :], in0=identf, scalar1=cw.t[:, cc, jj:jj + 1],
                                                            scalar2=None, op0=ALU.mult), cmat.b + cw.b, diag.b)

        rr = [0]

        def bank():
            rr[0] = (rr[0] + 1) % 4
            return pb[rr[0]]

        tiles = [(j, t) for j, jb in enumerate(jobs if LV >= 1 else []) for t in range(jb["n"])]

        def load_norm(idx):
            j, t = tiles[idx]
            kvf = kvf2[idx % 2]
            S.gbegin()
            for c2 in range(4):
                S.dma("sp", xt.t[:, 2 * c2:2 * c2 + 2, :], xt_in[j][t, :, 2 * c2:2 * c2 + 2, :], [], xt.b[2 * c2:2 * c2 + 2], "xt")
            S.gend()
            S.dma("sp", kvf.t[:], kv_in[j][t], [], kvf.b, f"kvf{idx % 2}")
            S.tag = "norm"
            rmsnorm_fm(lambda c: xt.t[:, c, :], xt.b, TW, [(0, 512), (512, 768)], 0, hT, sq, pb[6], pb[7], rstd)

        def stage_b(idx):
            j, t = tiles[idx]
            S.dma("sp", cs.t[:], cs_in[j][t], [], cs.b, "cs")
            S.tag = "glu"
            for cc in range(4):
                for half in range(2):
                    r0 = 112 + half * 272
                    vps, gps = (pb[0], pb[1]) if (cc * 2 + half) % 2 == 0 else (pb[2], pb[3])
                    for k in range(8):
                        mm(vps.t[:, 0:272], w0.t[:, k, cc * 128:(cc + 1) * 128], hT.t[:, k, r0:r0 + 272], k == 0, k == 7,
                           w0.b + hT.b, vps.b, k == 7)
                    for k in range(8):
                        mm(gps.t[:, 0:272], w0.t[:, k, 512 + cc * 128:512 + (cc + 1) * 128], hT.t[:, k, r0:r0 + 272], k == 0, k == 7,
                           w0.b + hT.b, gps.b, k == 7)
                    sgt = sg[half]
                    S.op("act", lambda: nc.scalar.activation(out=sgt.t[:, 0:272], in_=gps.t[:, 0:272], func=AF.Sigmoid), gps.b, sgt.b)
                    S.op("dve", lambda: nc.vector.tensor_tensor(out=a_sb.t[:, cc, half * 272:(half + 1) * 272], in0=vps.t[:, 0:272],
                                                                in1=sgt.t[:, 0:272], op=ALU.mult), vps.b + sgt.b, [a_sb.b[cc]])
            S.tag = "qk"
            qk_tasks = [("q", cc, H, H + T) for cc in range(4)] + [("k", 0, 0, 512), ("k", 1, 512, 768)]

            def qk_proj(ti):
                kind, idx_, a, b_ = qk_tasks[ti]
                qps = pb[ti % 4]
                c0 = 1536 + idx_ * 128 if kind == "q" else 2048
                for k in range(8):
                    mm(qps.t[:, 0:b_ - a], w0.t[:, k, c0:c0 + 128], hT.t[:, k, a:b_], k == 0, k == 7, w0.b + hT.b, qps.b, k == 7)

            qk_proj(0)
            for ti, (kind, idx_, a, b_) in enumerate(qk_tasks):
                if ti + 1 < len(qk_tasks):
                    qk_proj(ti + 1)
                if kind == "q":
                    rope(pb[ti % 4], 512, cs.t[:, 0, a:b_], cs.t[:, 1, a:b_], cs.b, qf[ti % 2], pb[4 + ti % 2], tmpf[ti % 2],
                         qr.t[:, idx_, :], [qr.b[idx_]])
                else:
                    rope(pb[ti % 4], b_ - a, cs.t[:, 0, a:b_], cs.t[:, 1, a:b_], cs.b, qf[ti % 2], pb[4 + ti % 2], tmpf[ti % 2], None, kr.b,
                         outs=[(slice(0, 64), kr.t[0:64, 0, a:b_]), (slice(64, 128), kr.t[64:128, 1, a:b_])])
            S.tag = "v"
            for blk in range(6):
                vps = bank()
                for k in range(8):
                    mm(vps.t[:, 0:128], hT.t[:, k, blk * 128:(blk + 1) * 128], w0.t[:, k, 2176:2304], k == 0, k == 7, w0.b + hT.b, vps.b, k == 7)
                S.op("act", lambda: nc.scalar.copy(out=vaug.t[:, blk, 0, 0:64], in_=vps.t[:, 0:64]), vps.b, vaug.b)
                S.op("dve", lambda: nc.vector.tensor_copy(out=vaug.t[:, blk, 1, 64:128], in_=vps.t[:, 64:128]), vps.b, vaug.b)
            S.tag = "gates"
            for cc in range(4):
                gps = bank()
                for k in range(8):
                    mm(gps.t[:], w0.t[:, k, 2304 + cc * 128:2304 + (cc + 1) * 128], hT.t[:, k, H:H + T], k == 0, k == 7, w0.b + hT.b, gps.b, k == 7)
                S.op("act", lambda: nc.scalar.activation(out=sbg.t[:, cc, :], in_=gps.t[:], func=AF.Silu), gps.b, [sbg.b[cc]])
            for cc in range(4):
                gps = bank()
                for k in range(8):
                    mm(gps.t[:], w0.t[:, k, 1024 + cc * 128:1024 + (cc + 1) * 128], hT.t[:, k, H:H + T], k == 0, k == 7, w0.b + hT.b, gps.b, k == 7)
                S.op("act", lambda: nc.scalar.activation(out=AO.t[:, cc, :], in_=gps.t[:], func=AF.Silu), gps.b, [AO.b[cc]])

        def stage_c(idx):
            j, t = tiles[idx]
            kvf = kvf2[idx % 2]
            S.tag = "conv"
            for cc in range(4):
                cps = pb[4 + cc % 2]
                for jj in range(31):
                    mm(cps.t[:], diag.t[:, cc * 31 + jj, :], a_sb.t[:, cc, jj + 1:jj + 1 + 512], jj == 0, jj == 30,
                       diag.b + [a_sb.b[cc]], cps.b, jj == 30)
                S.op("act", lambda: nc.scalar.activation(out=co.t[:, cc, :], in_=cps.t[:], func=AF.Identity,
                                                         bias=colp.t[:, 24 + cc:25 + cc], scale=1.0), cps.b + colp.b, [co.b[cc]])
                S.op("act", lambda: nc.scalar.activation(out=co2[cc % 2].t[:], in_=co.t[:, cc, :], func=AF.Square), [co.b[cc]], co2[cc % 2].b)
                S.op("dve", lambda: nc.vector.tensor_copy(out=cobf[cc % 2].t[:], in_=co.t[:, cc, :]), [co.b[cc]], cobf[cc % 2].b)
                mm(pb[6].t[:], onesb.t[:], cobf[cc % 2].t[:], cc == 0, cc == 3, cobf[cc % 2].b + onesb.b, pb[6].b, True)
                mm(pb[7].t[:], onesb.t[:], co2[cc % 2].t[:], cc == 0, cc == 3, co2[cc % 2].b + onesb.b, pb[7].b, True)
            S.op("dve", lambda: nc.vector.tensor_scalar(out=mean_sb.t[:], in0=pb[6].t[:], scalar1=1.0 / 512, scalar2=None, op0=ALU.mult),
                 pb[6].b, mean_sb.b)
            S.op("dve", lambda: nc.vector.tensor_tensor(out=rs2.t[:], in0=mean_sb.t[:], in1=mean_sb.t[:], op=ALU.mult), mean_sb.b, rs2.b)
            S.op("dve", lambda: nc.vector.scalar_tensor_tensor(out=rs2.t[:], in0=pb[7].t[:], scalar=1.0 / 512, in1=rs2.t[:],
                                                               op0=ALU.mult, op1=ALU.subtract), pb[7].b + rs2.b, rs2.b)

            def ln_stats():
                S.op("act", lambda: nc.scalar.activation(out=rs2.t[:], in_=rs2.t[:], func=AF.Ln, bias=epsc.t[:], scale=1.0), rs2.b + epsc.b, rs2.b)
                S.op("act", lambda: nc.scalar.activation(out=rs2.t[:], in_=rs2.t[:], func=AF.Exp, scale=-0.5), rs2.b, rs2.b)
            ei = 0
            S.tag = "att"
            def ln_apply(cc):
                S.op("dve", lambda: nc.vector.tensor_tensor(out=co.t[:, cc, :], in0=co.t[:, cc, :], in1=mean_sb.t[:], op=ALU.subtract),
                     [co.b[cc]] + mean_sb.b, [co.b[cc]])
                S.op("dve", lambda: nc.vector.tensor_tensor(out=co.t[:, cc, :], in0=co.t[:, cc, :], in1=rs2.t[:], op=ALU.mult),
                     [co.b[cc]] + rs2.b, [co.b[cc]])
                S.op("act", lambda: nc.scalar.activation(out=co.t[:, cc, :], in_=co.t[:, cc, :], func=AF.Silu,
                                                         bias=colp.t[:, 32 + cc:33 + cc], scale=colp.t[:, 28 + cc:29 + cc]),
                     [co.b[cc]] + colp.b, [co.b[cc]])
                S.op("dve", lambda: nc.vector.tensor_tensor(out=AO.t[:, cc, :], in0=co.t[:, cc, :], in1=AO.t[:, cc, :], op=ALU.mult),
                     [co.b[cc], AO.b[cc]], [AO.b[cc]])

            pending = []
            groups = [(qb, kh) for qb in range(4) for kh in range(2)]
            sset = [[pb[0], pb[1], pb[2]], [pb[3], pb[4], pb[5]]]

            def emit_s(gi):
                qb, kh = groups[gi]
                for jj in range(3):
                    blk = qb + jj
                    sps = sset[gi % 2][jj]
                    mm(sps.t[:], kr.t[:, kh, blk * 128:(blk + 1) * 128], qr.t[:, :, qb * 128:(qb + 1) * 128], True, True,
                       kr.b + qr.b, sps.b, True)

            emit_s(0)
            for gi, (qb, kh) in enumerate(groups):
                op_ = slice(kh * 64, (kh + 1) * 64)
                dp_ = slice((1 - kh) * 64, (2 - kh) * 64)
                ops_ = pb[6 + gi % 2]
                if gi + 1 < len(groups):
                    emit_s(gi + 1)
                Es = []
                for jj in range(3):
                    sps = sset[gi % 2][jj]
                    E = Et[jj]
                    Es.append(E)
                    S.op("act", lambda: nc.scalar.activation(out=E.t[:], in_=sps.t[:], func=AF.Exp, scale=0.125), sps.b, E.b)
                for jj in (0, 2):
                    E = Es[jj]
                    if jj == 0 and qb == 0:
                        sc = kvf.t[:, 0:1]
                    elif jj == 2 and qb == 3:
                        sc = kvf.t[:, 1:2]
                    else:
                        sc = 1.0
                    S.op("dve", lambda: nc.vector.scalar_tensor_tensor(out=E.t[:], in0=E.t[:], scalar=sc, in1=maskLR.t[:, jj // 2, :],
                                                                       op0=ALU.mult, op1=ALU.mult), E.b + maskLR.b + kvf.b, E.b)
                for jj in (1, 0, 2):
                    blk = qb + jj
                    mm(ops_.t[:], vaug.t[:, blk, kh, :], Es[jj].t[:], jj == 1, jj == 2, vaug.b + Es[jj].b, ops_.b, jj == 2)

                def normalize(ops_=ops_, op_=op_, dp_=dp_, qb=qb):
                    S.op("dve", lambda: nc.vector.tensor_tensor(out=rd.t[dp_, :].rearrange("p (g q) -> p g q", g=4),
                                                                in0=ops_.t[dp_, :].rearrange("p (g q) -> p g q", g=4),
                                                                in1=esink.t[dp_, :].unsqueeze(2).to_broadcast([64, 4, 128]), op=ALU.add),
                         ops_.b + esink.b, rd.b)
                    S.op("act", lambda: nc.scalar.activation(out=rd.t[dp_, :], in_=rd.t[dp_, :], func=AF.Ln), rd.b, rd.b)
                    S.op("act", lambda: nc.scalar.activation(out=rd.t[dp_, :], in_=rd.t[dp_, :], func=AF.Exp, scale=-1.0), rd.b, rd.b)
                    S.op("dve", lambda: nc.vector.tensor_tensor(out=rd.t[op_, :], in0=ops_.t[op_, :], in1=rd.t[dp_, :], op=ALU.mult),
                         ops_.b + rd.b, rd.b)
                    S.op("dve", lambda: nc.vector.tensor_tensor(out=AO.t[op_, 4:8, qb * 128:(qb + 1) * 128],
                                                                in0=rd.t[op_, :].rearrange("p (g q) -> p g q", g=4),
                                                                in1=sbg.t[op_, :, qb * 128:(qb + 1) * 128], op=ALU.mult),
                         rd.b + sbg.b, AO.b[4:8])

                for fn in pending:
                    fn()
                pending = [normalize]
                if gi == 1:
                    pending.append(ln_stats)
                if gi == 4:
                    for cc_ in range(4):
                        pending.append(lambda cc_=cc_: ln_apply(cc_))
            for fn in pending:
                fn()
            S.tag = "outp"
            for oc in range(8):
                xc = xc2[(oc // 2) % 2]
                if oc % 2 == 0:
                    S.dma("sp", xc.t[:], xt_in[j][t, :, oc:oc + 2, H:H + T], [], xc.b, f"xc{(oc // 2) % 2}")
                yps = pb[oc % 4]
                for ki, k in enumerate([4, 5, 6, 7, 0, 1, 2, 3]):
                    mm(yps.t[:], wo.t[:, k, oc * 128:(oc + 1) * 128], AO.t[:, k, :], ki == 0, ki == 7, wo.b + [AO.b[k]], yps.b, ki == 7)
                S.op("dve", lambda: nc.vector.tensor_tensor(out=xc.t[:, oc % 2, :], in0=yps.t[:], in1=xc.t[:, oc % 2, :], op=ALU.add),
                     yps.b + xc.b, xc.b)
                if oc % 2 == 1:
                    S.dma("pool", x1T[j][t, :, oc - 1:oc + 1, :], xc.t[:], xc.b, [DB("x1T", j, t)], f"x1w{(oc // 2) % 2}")

        if tiles:
            load_norm(0)
        for idx in range(len(tiles)):
            stage_b(idx)
            if idx + 1 < len(tiles):
                load_norm(idx + 1)
            stage_c(idx)
        S.barrier()

    W1 = 516
    with ExitStack() as ph:
        w1 = sbt(ph, "w1", [128, 8, 3600], BF16)
        diag5 = sbt(ph, "diag5", [128, 40, 128], BF16)
        x1t = sbt(ph, "x1t", [128, 8, W1], F32, 8)
        stg = sbt(ph, "stg", [128, 8, 1024], F32, 8)
        cs = sbt(ph, "cs1", [128, 2, TW], F32)
        hfl = sbt(ph, "hfl", [128, 2], F32)
        sq = [sbt(ph, f"sq1{i}", [128, W1], BF16) for i in range(2)]
        rstd = sbt(ph, "rstd1", [128, W1], F32)
        hT = sbt(ph, "hT1", [128, 8, W1], BF16)
        qf = [sbt(ph, f"qf1{i}", [128, 512], F32) for i in range(2)]
        tmpf = [sbt(ph, f"tmpf1{i}", [128, 512], F32) for i in range(2)]
        qr1 = sbt(ph, "qr1", [128, 4, 512], BF16)
        kr1 = sbt(ph, "kr1", [128, 4, 512], BF16)
        vt1 = sbt(ph, "vt1", [128, 4, 512], BF16)
        cgt = sbt(ph, "cgt", [128, 4, 512], BF16)
        zt = sbt(ph, "zt", [128, 4, 512], BF16)
        xr = sbt(ph, "xr", [128, 8, W1], BF16, 8)
        xa = sbt(ph, "xa", [128, 8, 512], BF16, 8)
        xb = sbt(ph, "xb", [128, 4, 768], BF16)
        dta = sbt(ph, "dta", [128, 4, 32], F32)
        dtmp = sbt(ph, "dtmp", [128, 16], F32)
        pb = [pst(ph, f"pc{i}", [128, 512], F32) for i in range(7)]
        ptb = pst(ph, "ptb", [128, 1024], BF16)

        load_weight(w1, w_in1, 3600, stg)
        for c8 in range(8):
            for jj in range(5):
                S.op("dve", lambda c8=c8, jj=jj: nc.vector.tensor_scalar(out=diag5.t[:, c8 * 5 + jj, :], in0=identf,
                                                                         scalar1=scw.t[:, c8, jj:jj + 1], scalar2=None, op0=ALU.mult),
                     cmat.b + scw.b, diag5.b)
        rr = [0]

        def bank():
            rr[0] = (rr[0] + 1) % 2
            return pb[rr[0]]

        tilesb = [(j, t) for j, jb in enumerate(jobs if LV >= 2 else []) for t in range(jb["n"])]

        def b_load_norm(idx):
            j, t = tilesb[idx]
            n, nq = jobs[j]["n"], jobs[j]["nq"]
            isq = t < nq
            S.gbegin()
            for c2 in range(4):
                S.dma("sp", x1t.t[:, 2 * c2:2 * c2 + 2, 2:514], x1T[j][t, :, 2 * c2:2 * c2 + 2, :], [DB("x1T", j, t)], x1t.b[2 * c2:2 * c2 + 2], "x1t")
            S.dma("sp", x1t.t[:, :, 0:2], x1T[j][(t - 1) % n, :, :, 510:512], [DB("x1T", j, (t - 1) % n)], x1t.b, "x1t")
            S.dma("sp", x1t.t[:, :, 514:516], x1T[j][(t + 1) % n, :, :, 0:2], [DB("x1T", j, (t + 1) % n)], x1t.b, "x1t")
            S.gend()
            S.dma("sp", cs.t[:], cs_in[j][t], [], cs.b, "cs")
            S.dma("sp", hfl.t[:], hf_in[j][t], [], hfl.b, "hfl")
            S.op("dve", lambda: nc.vector.tensor_scalar(out=x1t.t[:, :, 0:2], in0=x1t.t[:, :, 0:2], scalar1=hfl.t[:, 0:1], scalar2=None,
                                                        op0=ALU.mult), x1t.b + hfl.b, x1t.b)
            S.op("dve", lambda: nc.vector.tensor_scalar(out=x1t.t[:, :, 514:516], in0=x1t.t[:, :, 514:516], scalar1=hfl.t[:, 1:2], scalar2=None,
                                                        op0=ALU.mult), x1t.b + hfl.b, x1t.b)
            rmsnorm_fm(lambda c: x1t.t[:, c, :], x1t.b, W1, [(0, 258), (258, 516)], 8, hT, sq, pb[2], pb[3], rstd)

        def b_proj(idx):
            j, t = tilesb[idx]
            n, nq = jobs[j]["n"], jobs[j]["nq"]
            isq = t < nq
            cosc, sinc = cs.t[:, 0, H:H + T], cs.t[:, 1, H:H + T]
            tasks = [(which, hh) for which, on in ((0, isq), (1, True)) if on for hh in range(4)]

            def qk_proj(ti):
                which, hh = tasks[ti]
                qps = pb[ti % 2]
                c0 = which * 512 + hh * 128
                for k in range(8):
                    mm(qps.t[:], w1.t[:, k, c0:c0 + 128], hT.t[:, k, 2:514], k == 0, k == 7, w1.b + hT.b, qps.b, k == 7)

            qk_proj(0)
            for ti, (which, hh) in enumerate(tasks):
                if ti + 1 < len(tasks):
                    qk_proj(ti + 1)
                dstt = qr1 if which == 0 else kr1
                rope(pb[ti % 2], 512, cosc, sinc, cs.b, qf[ti % 2], pb[4 + ti % 2], tmpf[ti % 2], dstt.t[:, hh, :], dstt.b)
                if hh == 3:
                    if which == 0:
                        S.dma("pool", QT[j][t], qr1.t[:], qr1.b, [DB("QT", j, t)], "wq")
                    else:
                        S.dma("pool", KT[j][t], kr1.t[:], kr1.b, [DB("KT", j, t)], "wk")
            for blk in range(4):
                vps = bank()
                for k in range(8):
                    mm(vps.t[:], hT.t[:, k, 2 + blk * 128:2 + (blk + 1) * 128], w1.t[:, k, 1024:1536], k == 0, k == 7, w1.b + hT.b, vps.b, k == 7)
                S.op("act", lambda: nc.scalar.copy(out=vt1.t[:, blk, :], in_=vps.t[:]), vps.b, vt1.b)
            S.dma("pool", VT[j][t], vt1.t[:], vt1.b, [DB("VT", j, t)], "wv")
            if isq:
                for blk in range(4):
                    gps = bank()
                    for k in range(8):
                        mm(gps.t[:], hT.t[:, k, 2 + blk * 128:2 + (blk + 1) * 128], w1.t[:, k, 1536:2048], k == 0, k == 7, w1.b + hT.b, gps.b, k == 7)
                    S.op("act", lambda: nc.scalar.activation(out=cgt.t[:, blk, :], in_=gps.t[:], func=AF.Silu), gps.b, cgt.b)
                S.dma("pool", CG[j][t], cgt.t[:], cgt.b, [DB("CG", j, t)], "wcg")
                for blk in range(4):
                    zps = bank()
                    for k in range(8):
                        mm(zps.t[:], hT.t[:, k, 2 + blk * 128:2 + (blk + 1) * 128], w1.t[:, k, 2048:2560], k == 0, k == 7, w1.b + hT.b, zps.b, k == 7)
                    S.op("act", lambda: nc.scalar.activation(out=zt.t[:, blk, :], in_=zps.t[:], func=AF.Silu), zps.b, zt.b)
                S.dma("pool", ZS[j][t], zt.t[:], zt.b, [DB("ZS", j, t)], "wz")
            for c8 in range(8):
                for half in range(2):
                    xps = bank()
                    for k in range(8):
                        mm(xps.t[:, 0:258], w1.t[:, k, 2560 + c8 * 128:2560 + (c8 + 1) * 128], hT.t[:, k, half * 258:(half + 1) * 258], k == 0, k == 7,
                           w1.b + hT.b, xps.b, k == 7)
                    if half == 0:
                        S.op("act", lambda: nc.scalar.copy(out=xr.t[:, c8, 0:258], in_=xps.t[:, 0:258]), xps.b, [xr.b[c8]])
                    else:
                        S.op("dve", lambda: nc.vector.tensor_copy(out=xr.t[:, c8, 258:516], in_=xps.t[:, 0:258]), xps.b, [xr.b[c8]])
            for blk in range(4):
                dps = pb[6]
                for k in range(8):
                    mm(dps.t[:, 0:16], hT.t[:, k, 2 + blk * 128:2 + (blk + 1) * 128], w1.t[:, k, 3584:3600], k == 0, k == 7, w1.b + hT.b, dps.b, k == 7)
                S.op("dve", lambda: nc.vector.tensor_tensor(out=dtmp.t[:], in0=dps.t[:, 0:16], in1=rowp.t[:, 256:272], op=ALU.add),
                     dps.b + rowp.b, dtmp.b)
                S.op("act", lambda: nc.scalar.activation(out=dtmp.t[:], in_=dtmp.t[:], func=AF.Exp), dtmp.b, dtmp.b)
                S.op("act", lambda: nc.scalar.activation(out=dta.t[:, blk, 0:16], in_=dtmp.t[:], func=AF.Ln, bias=onec.t[:], scale=1.0), dtmp.b + onec.b, dta.b)
                S.op("dve", lambda: nc.vector.tensor_tensor(out=dta.t[:, blk, 16:32], in0=dta.t[:, blk, 0:16], in1=abc.t[:], op=ALU.mult),
                     dta.b + abc.b, dta.b)
            S.dma("pool", DTA[j][t], dta.t[:], dta.b, [DB("DTA", j, t)], "wdta")

        def b_post(idx):
            j, t = tilesb[idx]
            n, nq = jobs[j]["n"], jobs[j]["nq"]
            isq = t < nq
            for c8 in range(8):
                cps = pb[4 + c8 % 2]
                for jj in range(5):
                    mm(cps.t[:], diag5.t[:, c8 * 5 + jj, :], xr.t[:, c8, jj:jj + 512], jj == 0, jj == 4, diag5.b + [xr.b[c8]], cps.b, jj == 4)
                S.op("act", lambda: nc.scalar.activation(out=xa.t[:, c8, :], in_=cps.t[:], func=AF.Silu, bias=colp.t[:, 40 + c8:41 + c8], scale=1.0),
                     cps.b + colp.b, [xa.b[c8]])
            if isq:
                S.dma("pool", BCT[j][t], xa.t[:, 4:8, :], xa.b[4:8], [DB("BCT", j, t)], "wbc")
            for blk in range(4):
                for c6 in range(6):
                    S.op("pe", lambda: nc.tensor.transpose(out=ptb.t[:, c6 * 128:(c6 + 1) * 128], in_=xa.t[:, c6, blk * 128:(blk + 1) * 128],
                                                           identity=identb.t[:]), [xa.b[c6]] + identb.b, ptb.b, inc=(c6 == 5))
                S.op("dve", lambda: nc.vector.tensor_copy(out=xb.t[:, blk, :], in_=ptb.t[:, 0:768]), ptb.b, xb.b)
            S.dma("pool", XB[j][t], xb.t[:], xb.b, [DB("XB", j, t)], "wxb")

        if tilesb:
            b_load_norm(0)
        for idx in range(len(tilesb)):
            b_proj(idx)
            if idx + 1 < len(tilesb):
                b_load_norm(idx + 1)
            b_post(idx)
        S.barrier()

    with ExitStack() as ph:
        NB = 3
        xbk = [sbt(ph, f"xbk{i}", [128, 768], BF16) for i in range(NB)]
        dtk = [sbt(ph, f"dtk{i}", [128, 32], F32) for i in range(NB)]
        kfl = sbt(ph, "kfl", [128, 2, 64], F32)
        carry = sbt(ph, "carry", [128, 512], F32)
        sm = [sbt(ph, f"sm{i}", [128, 48], F32) for i in range(2)]
        xw = [sbt(ph, f"xw{i}", [128, 512], BF16) for i in range(2)]
        pvb_ = [sbt(ph, f"pvo{i}", [128, 512], BF16) for i in range(2)]
        pss = [pst(ph, f"p2s{i}", [128, 512], F32) for i in range(2)]
        psm = [pst(ph, f"p2m{i}", [128, 16], F32) for i in range(2)]
        it = 0
        for j, jb in enumerate(jobs if LV >= 3 else []):
            n, nq = jb["n"], jb["nq"]
            NC = n * 4
            S.dma("sp", kfl.t[:, :, 0:NC], kf_in[j], [], kfl.b, "kfl")
            for d in range(2):
                order = [(nq * 4 + i) % NC for i in range(NC)] if d == 0 else list(range(NC - 1, -1, -1))
                S.op("dve", lambda: nc.vector.memset(carry.t[:], 0.0), [], carry.b)
                for c in order:
                    tt, blk = c // 4, c % 4
                    xk, dk, s_, xw_, pv_, ps_, pm_ = xbk[it % NB], dtk[it % NB], sm[it % 2], xw[it % 2], pvb_[it % 2], pss[it % 2], psm[it % 2]
                    it += 1
                    S.gbegin()
                    S.dma("sp", xk.t[:], XB[j][tt, :, blk, :], [DB("XB", j, tt)], xk.b, f"p2x{it % NB}")
                    S.dma("sp", dk.t[:], DTA[j][tt, :, blk, :], [DB("DTA", j, tt)], dk.b, f"p2x{it % NB}")
                    S.gend()
                    a_ap = dk.t[:, 16 + 8 * d:24 + 8 * d]
                    dt_ap = dk.t[:, 8 * d:8 * d + 8]
                    mm(pm_.t[:, 0:8], triD[d], a_ap, True, True, cmat.b + dk.b, pm_.b, False)
                    mm(pm_.t[:, 8:16], onesf, a_ap, True, True, cmat.b + dk.b, pm_.b, True)
                    S.op("act", lambda: nc.scalar.copy(out=s_.t[:, 0:8], in_=pm_.t[:, 8:16]), pm_.b, s_.b)
                    S.op("dve", lambda: nc.vector.tensor_tensor(out=s_.t[:, 8:16], in0=s_.t[:, 0:8], in1=pm_.t[:, 0:8], op=ALU.subtract),
                         s_.b + pm_.b, s_.b)
                    S.op("act", lambda: nc.scalar.activation(out=s_.t[:, 8:16], in_=s_.t[:, 8:16], func=AF.Exp), s_.b, s_.b)
                    S.op("act", lambda: nc.scalar.activation(out=s_.t[:, 16:24], in_=s_.t[:, 0:8], func=AF.Exp), s_.b, s_.b)
                    S.op("dve", lambda: nc.vector.tensor_tensor(out=s_.t[:, 8:16], in0=s_.t[:, 8:16], in1=dt_ap, op=ALU.mult), s_.b + dk.b, s_.b)
                    S.op("dve", lambda: nc.vector.tensor_scalar(out=s_.t[:, 16:24], in0=s_.t[:, 16:24], scalar1=kfl.t[:, d, c:c + 1], scalar2=None,
                                                                op0=ALU.mult), s_.b + kfl.b, s_.b)
                    S.op("dve", lambda: nc.vector.tensor_tensor(out=xw_.t[:].rearrange("p (h d) -> p h d", h=8),
                                                                in0=xk.t[:, 0:512].rearrange("p (h d) -> p h d", h=8),
                                                                in1=s_.t[:, 8:16].unsqueeze(2).to_broadcast([128, 8, 64]), op=ALU.mult),
                         xk.b + s_.b, xw_.b)
                    for g in range(2):
                        mm(ps_.t[:, g * 256:(g + 1) * 256], xk.t[:, 512 + g * 128:512 + (g + 1) * 128], xw_.t[:, g * 256:(g + 1) * 256], True, True,
                           xk.b + xw_.b, ps_.b, g == 1)
                    if c < nq * 4 or dbg:
                        S.op("act", lambda: nc.scalar.activation(out=pv_.t[:], in_=carry.t[:], func=AF.Identity, scale=kfl.t[:, d, c:c + 1]),
                             carry.b + kfl.b, pv_.b)
                        S.dma("pool", PRV[j][d, c], pv_.t[:], pv_.b, [DB("PRV", j, (d, c))], f"w2{it % 2}")
                    S.op("dve", lambda: nc.vector.tensor_tensor(out=carry.t[:].rearrange("p (h d) -> p h d", h=8),
                                                                in0=carry.t[:].rearrange("p (h d) -> p h d", h=8),
                                                                in1=s_.t[:, 16:24].unsqueeze(2).to_broadcast([128, 8, 64]), op=ALU.mult),
                         carry.b + s_.b, carry.b)
                    S.op("dve", lambda: nc.vector.tensor_tensor(out=carry.t[:], in0=carry.t[:], in1=ps_.t[:], op=ALU.add), carry.b + ps_.b, carry.b)
        S.barrier()

    with ExitStack() as ph:
        wo1 = sbt(ph, "wo1", [128, 8, 1024], BF16)
        stg = sbt(ph, "stg3", [128, 8, 512], F32, 8)
        qz = sbt(ph, "qz", [128, 2, 4, 512], BF16)
        cgt = sbt(ph, "cgt3", [128, 4, 512], BF16)
        NKB = 3
        kth = [sbt(ph, f"kth{i}", [128, 512], BF16) for i in range(NKB)]
        vth = [sbt(ph, f"vth{i}", [128, 4, 129], BF16) for i in range(NKB)]
        den8 = sbt(ph, "den8", [128, 8], F32)
        r1n = sbt(ph, "r1n", [128, 4], F32)
        o_t = sbt(ph, "o_t", [128, 4, 128], F32)
        sq4 = sbt(ph, "sq4", [128, 4, 128], F32)
        ssq4 = sbt(ph, "ssq4", [128, 8], F32)
        ctok = sbt(ph, "ctok", [128, 4, 128], BF16)
        grow = sbt(ph, "grow", [128, 128], F32)
        accs = sbt(ph, "accs", [128, 8, 129], F32)
        Et = [sbt(ph, f"E3{i}", [128, 1024], BF16) for i in range(3)]
        fa = [sbt(ph, f"fa{i}", [128, 512], F32) for i in range(4)]
        osq = sbt(ph, "osq", [128, 512], BF16)
        CD = sbt(ph, "CD", [128, 8, 512], BF16, 8)
        x1c = sbt(ph, "x1c", [128, 8, 512], F32, 8)
        sqf = [sbt(ph, f"sqf{i}", [128, 512], BF16) for i in range(2)]
        rstd = sbt(ph, "rstd3", [128, 512], F32)
        class SS:
            pass
        sset3 = []
        for p_ in range(2):
            X = SS()
            X.xbk = sbt(ph, f"xbk3{p_}", [128, 768], BF16)
            X.dtk = sbt(ph, f"dtk3{p_}", [128, 32], F32)
            X.bct = sbt(ph, f"bct{p_}", [128, 4, 128], BF16)
            X.pv = [sbt(ph, f"pv3{p_}{i}", [128, 512], BF16) for i in range(2)]
            X.zsk = sbt(ph, f"zsk{p_}", [128, 512], BF16)
            X.G_sb = sbt(ph, f"G_sb{p_}", [128, 2, 128], F32)
            X.smc = [sbt(ph, f"smc{p_}{i}", [128, 24], F32) for i in range(2)]
            X.ahl = [sbt(ph, f"ahl{p_}{i}", [128, 2, 8], BF16) for i in range(2)]
            X.rhs = [[sbt(ph, f"rhs{p_}{d}{k}", [128, 8, 128], BF16) for k in range(2)] for d in range(2)]
            X.Mt = [sbt(ph, f"Mt{p_}{i}", [128, 8, 128], BF16) for i in range(2)]
            X.yo = [sbt(ph, f"yo{p_}{i}", [128, 512], F32) for i in range(2)]
            X.ysum = sbt(ph, f"ysum{p_}", [128, 512], F32)
            X.dtok = sbt(ph, f"dtok{p_}", [128, 512], BF16)
            sset3.append(X)
        Eh = [sbt(ph, f"Eh{i}", [128, 128], F32) for i in range(4)]
        ssq = sbt(ph, "ssq", [128, 2], F32)
        gssm = sbt(ph, "gssm", [128, 512], F32)
        pbA = [pst(ph, f"pd{i}", [128, 512], F32) for i in range(4)]
        SP = [pst(ph, f"sp{i}", [128, 1024], F32) for i in range(2)]
        ptb_ap = pbA[3].t[:].bitcast(BF16)

        load_weight(wo1, w_out1, 1024, stg)
        S.op("pool", lambda: nc.gpsimd.memset(qz.t[:], 0.0), [], qz.b)
        for i in range(NKB):
            S.op("pool", lambda: nc.gpsimd.memset(vth[i].t[:, :, 128:129], 1.0), [], vth[i].b)
        S.op("dve", lambda: nc.vector.tensor_scalar(out=grow.t[:], in0=rowp.t[:, 296:424], scalar1=1.0 - LAM_INIT, scalar2=None, op0=ALU.mult),
             rowp.b, grow.b)
        S.dma("sp", gssm.t[:], rowp_in[:, 512:1024].partition_broadcast(128), [], gssm.b, "gssm")
        lk = [0]
        for j, jb in enumerate(jobs if LV >= 4 else []):
            n, nq = jb["n"], jb["nq"]
            for t in range(nq):
                S.gbegin()
                S.dma("sp", qz.t[0:64, 0, :, :], QT[j][t, 0:64], [DB("QT", j, t)], qz.b, "p3q")
                S.dma("sp", qz.t[64:128, 1, :, :], QT[j][t, 64:128], [DB("QT", j, t)], qz.b, "p3q")
                S.dma("sp", cgt.t[:], CG[j][t], [DB("CG", j, t)], cgt.b, "p3q")
                S.gend()
                S.gbegin()
                for c2 in range(4):
                    S.dma("sp", x1c.t[:, 2 * c2:2 * c2 + 2, :], x1T[j][t, :, 2 * c2:2 * c2 + 2, :], [DB("x1T", j, t)], x1c.b[2 * c2:2 * c2 + 2], "p3x")
                S.gend()
                def ssd_loads(blk):
                    X = sset3[blk % 2]
                    c = t * 4 + blk
                    sn = f"p3s{blk % 2}"
                    S.gbegin()
                    S.dma("sp", X.xbk.t[:], XB[j][t, :, blk, :], [DB("XB", j, t)], X.xbk.b, sn)
                    S.dma("sp", X.dtk.t[:], DTA[j][t, :, blk, :], [DB("DTA", j, t)], X.dtk.b, sn)
                    S.dma("sp", X.bct.t[:], BCT[j][t, :, :, blk * 128:(blk + 1) * 128], [DB("BCT", j, t)], X.bct.b, sn)
                    for d in range(2):
                        S.dma("sp", X.pv[d].t[:], PRV[j][d, c], [DB("PRV", j, (d, c))], X.pv[d].b, sn)
                    S.dma("sp", X.zsk.t[:], ZS[j][t, :, blk, :], [DB("ZS", j, t)], X.zsk.b, sn)
                    S.gend()

                ssd_loads(0)
                S.tag = "datt"
                def acc_ap(r):
                    return pbA[r // 3].t[:, (r % 3) * 129:(r % 3) * 129 + 129]

                stream = [(hh_, kt_) for hh_ in range(4) for kt_ in range(n)]
                kbufs = {}

                def load_kv(i_):
                    hh_, kt_ = stream[i_]
                    i = lk[0] % NKB
                    lk[0] += 1
                    S.gbegin()
                    S.dma("sp", kth[i].t[:], KT[j][kt_, :, hh_, :], [DB("KT", j, kt_)], kth[i].b, f"p3k{i}")
                    S.dma("sp", vth[i].t[:, :, 0:128], VT[j][kt_, :, :, hh_ * 128:(hh_ + 1) * 128], [DB("VT", j, kt_)], vth[i].b, f"p3k{i}")
                    S.gend()
                    kbufs[(hh_, kt_)] = i

                load_kv(0)
                if len(stream) > 1:
                    load_kv(1)

                for hh in range(4):
                    steps2 = [(kt, kb) for kt in range(n) for kb in range(4)]

                    def emit_s(si):
                        kt, kb = steps2[si]
                        ki = kbufs[(hh, kt)]
                        sp = SP[si % 2]
                        for t2 in range(2):
                            mm(sp.t[:, t2 * 512:(t2 + 1) * 512], kth[ki].t[:, kb * 128:(kb + 1) * 128], qz.t[:, t2, hh, :], True, True,
                               kth[ki].b + qz.b, sp.b, t2 == 1)

                    emit_s(0)
                    for si in range(len(steps2)):
                        kt, kb = steps2[si]
                        if kb == 0 and hh * n + kt + 2 < len(stream):
                            load_kv(hh * n + kt + 2)
                        if si + 1 < len(steps2):
                            emit_s(si + 1)
                        sp = SP[si % 2]
                        E = Et[si % 3]
                        S.op("act", lambda: nc.scalar.activation(out=E.t[:], in_=sp.t[:], func=AF.Exp, scale=0.125), sp.b, E.b)
                        first = (kt == 0 and kb == 0)
                        last = (kt == n - 1 and kb == 3)
                        ki = kbufs[(hh, kt)]
                        for r in range(8):
                            t2, qb = r // 4, r % 4
                            mm(acc_ap(r), E.t[:, t2 * 512 + qb * 128:t2 * 512 + (qb + 1) * 128], vth[ki].t[:, kb, :],
                               first and (r % 3 == 0), last, vth[ki].b + E.b, pbA[r // 3].b, r == 7)
                    for b3 in range(3):
                        nr = 3 if b3 < 2 else 2
                        S.op("dve", lambda: nc.vector.tensor_copy(out=accs.t[:, 3 * b3:3 * b3 + nr, :].rearrange("p r c -> p (r c)"),
                                                                  in_=pbA[b3].t[:, 0:129 * nr]), pbA[b3].b, accs.b)
                    accb = accs.b
                    S.op("act", lambda: nc.scalar.activation(out=den8.t[:], in_=accs.t[:, :, 128], func=AF.Ln), accb, den8.b)
                    S.op("act", lambda: nc.scalar.activation(out=den8.t[:], in_=den8.t[:], func=AF.Exp, scale=-1.0), den8.b, den8.b)
                    S.op("dve", lambda: nc.vector.tensor_scalar(out=r1n.t[:], in0=den8.t[:, 4:8], scalar1=neglam.t[:, 0:1], scalar2=None, op0=ALU.mult),
                         den8.b + neglam.b, r1n.b)
                    for qb in range(4):
                        S.op("dve", lambda: nc.vector.tensor_scalar(out=o_t.t[:, qb, :], in0=accs.t[:, qb, 0:128], scalar1=den8.t[:, qb:qb + 1], scalar2=None,
                                                                    op0=ALU.mult), accb + den8.b, o_t.b)
                        S.op("dve", lambda: nc.vector.scalar_tensor_tensor(out=o_t.t[:, qb, :], in0=accs.t[:, 4 + qb, 0:128], scalar=r1n.t[:, qb:qb + 1],
                                                                           in1=o_t.t[:, qb, :], op0=ALU.mult, op1=ALU.add), accb + r1n.b + o_t.b, o_t.b)
                    S.op("act", lambda: nc.scalar.activation(out=sq4.t[:], in_=o_t.t[:], func=AF.Square), o_t.b, sq4.b)
                    S.op("dve", lambda: nc.vector.tensor_reduce(out=ssq4.t[:, 0:4], in_=sq4.t[:], axis=mybir.AxisListType.X, op=ALU.add), sq4.b, ssq4.b)
                    S.op("act", lambda: nc.scalar.activation(out=ssq4.t[:, 4:8], in_=ssq4.t[:, 0:4], func=AF.Ln, bias=epsc.t[:], scale=1.0 / 128),
                         ssq4.b + epsc.b, ssq4.b)
                    S.op("act", lambda: nc.scalar.activation(out=ssq4.t[:, 4:8], in_=ssq4.t[:, 4:8], func=AF.Exp, scale=-0.5), ssq4.b, ssq4.b)
                    S.op("dve", lambda: nc.vector.tensor_tensor(out=o_t.t[:], in0=o_t.t[:], in1=ssq4.t[:, 4:8].unsqueeze(2).to_broadcast([128, 4, 128]),
                                                                op=ALU.mult), o_t.b + ssq4.b, o_t.b)
                    S.op("dve", lambda: nc.vector.tensor_tensor(out=o_t.t[:], in0=o_t.t[:], in1=grow.t[:].unsqueeze(1).to_broadcast([128, 4, 128]),
                                                                op=ALU.mult), o_t.b + grow.b, o_t.b)
                    S.op("dve", lambda: nc.vector.tensor_tensor(out=ctok.t[:], in0=o_t.t[:], in1=cgt.t[:, :, hh * 128:(hh + 1) * 128], op=ALU.mult),
                         o_t.b + cgt.b, ctok.b)
                    for qb in range(4):
                        S.op("pe", lambda: nc.tensor.transpose(out=ptb_ap[:, qb * 128:(qb + 1) * 128], in_=ctok.t[:, qb, :], identity=identb.t[:]),
                             ctok.b + identb.b, pbA[3].b, inc=(qb == 3))
                    S.op("act", lambda: nc.scalar.copy(out=CD.t[:, hh, :], in_=ptb_ap[:, 0:512]), pbA[3].b, [CD.b[hh]])
                S.tag = "ssd"

                def ssd_front(blk):
                    X = sset3[blk % 2]
                    gps = pbA[0]
                    for g in range(2):
                        mm(gps.t[:, g * 128:(g + 1) * 128], X.bct.t[:, g, :], X.bct.t[:, 2 + g, :], True, True, X.bct.b, gps.b, g == 1)
                    S.op("act", lambda: nc.scalar.copy(out=X.G_sb.t[:].rearrange("p g l -> p (g l)"), in_=gps.t[:, 0:256]), gps.b, X.G_sb.b)
                    cps = pbA[2]
                    for d in range(2):
                        a_ap = X.dtk.t[:, 16 + 8 * d:24 + 8 * d]
                        mm(cps.t[:, 8 * d:8 * d + 8], triD[d], a_ap, True, True, cmat.b + X.dtk.b, cps.b, True)
                    for d in range(2):
                        a_ap = X.dtk.t[:, 16 + 8 * d:24 + 8 * d]
                        sc_ = X.smc[d]
                        S.op("dve", lambda: nc.vector.tensor_scalar(out=sc_.t[:, 0:8], in0=cps.t[:, 8 * d:8 * d + 8], scalar1=-1.0, scalar2=None,
                                                                    op0=ALU.mult), cps.b, sc_.b)
                        S.op("act", lambda: nc.scalar.activation(out=sc_.t[:, 8:16], in_=cps.t[:, 8 * d:8 * d + 8], func=AF.Exp), cps.b, sc_.b)
                        ahl = X.ahl[d]
                        S.op("dve", lambda: nc.vector.tensor_copy(out=ahl.t[:, 0, :], in_=a_ap), X.dtk.b, ahl.b)
                        S.op("dve", lambda: nc.vector.tensor_tensor(out=ahl.t[:, 1, :], in0=a_ap, in1=ahl.t[:, 0, :], op=ALU.subtract),
                             X.dtk.b + ahl.b, ahl.b)
                        for k in range(2):
                            S.op("dve", lambda: nc.vector.tensor_tensor(out=X.rhs[d][k].t[:], in0=triD[d].unsqueeze(1).to_broadcast([128, 8, 128]),
                                                                        in1=ahl.t[:, k, :].unsqueeze(2).to_broadcast([128, 8, 128]), op=ALU.mult),
                                 cmat.b + ahl.b, X.rhs[d][k].b)
                        ab = SP[d]
                        for half in range(2):
                            abh = ab.t[:, half * 512:(half + 1) * 512]
                            mm(abh, onesb.t[:], X.rhs[d][0].t[:, half * 4:(half + 1) * 4, :], True, False, onesb.b + X.rhs[d][0].b, ab.b, False)
                            mm(abh, onesb.t[:], X.rhs[d][1].t[:, half * 4:(half + 1) * 4, :], False, False, onesb.b + X.rhs[d][1].b, ab.b, False)
                            mm(abh, identb.t[:], mbias.t[:, d, :], False, True, identb.b + mbias.b, ab.b, True)

                def ssd_mid(blk):
                    X = sset3[blk % 2]
                    for d in range(2):
                        dt_ap = X.dtk.t[:, 8 * d:8 * d + 8]
                        sc_ = X.smc[d]
                        ab = SP[d]
                        for h in range(8):
                            e_ = Eh[h % 4]
                            S.op("act", lambda: nc.scalar.activation(out=e_.t[:], in_=ab.t[:, h * 128:(h + 1) * 128], func=AF.Exp,
                                                                     bias=sc_.t[:, h:h + 1], scale=1.0), ab.b + sc_.b, e_.b)
                            S.op("dve", lambda: nc.vector.scalar_tensor_tensor(out=X.Mt[d].t[:, h, :], in0=e_.t[:], scalar=dt_ap[:, h:h + 1],
                                                                               in1=X.G_sb.t[:, h // 4, :], op0=ALU.mult, op1=ALU.mult),
                                 e_.b + X.dtk.b + X.G_sb.b, X.Mt[d].b)

                def ssd_back(blk):
                    X = sset3[blk % 2]
                    yps = pbA[1]
                    for d in range(2):
                        sc_ = X.smc[d]
                        for h in range(8):
                            mm(yps.t[:, h * 64:(h + 1) * 64], X.Mt[d].t[:, h, :], X.xbk.t[:, h * 64:(h + 1) * 64], d == 0 and h == 0, False,
                               X.Mt[d].b + X.xbk.b, yps.b, False)
                        ops_ = pbA[0]
                        for g in range(2):
                            mm(ops_.t[:, g * 256:(g + 1) * 256], X.bct.t[:, 2 + g, :], X.pv[d].t[:, g * 256:(g + 1) * 256], True, True,
                               X.bct.b + X.pv[d].b, ops_.b, g == 1)
                        S.op("dve", lambda: nc.vector.tensor_tensor(out=X.yo[d].t[:].rearrange("p (h d) -> p h d", h=8),
                                                                    in0=ops_.t[:].rearrange("p (h d) -> p h d", h=8),
                                                                    in1=sc_.t[:, 8:16].unsqueeze(2).to_broadcast([128, 8, 64]), op=ALU.mult),
                             ops_.b + sc_.b, X.yo[d].b)
                    for h in range(8):
                        mm(yps.t[:, h * 64:(h + 1) * 64], identD.t[:, h, :], X.xbk.t[:, h * 64:(h + 1) * 64], False, True, identD.b + X.xbk.b, yps.b, h == 7)
                    ysum, yo, dtok = X.ysum, X.yo, X.dtok
                    S.op("dve", lambda: nc.vector.tensor_tensor(out=ysum.t[:], in0=yps.t[:], in1=yo[0].t[:], op=ALU.add), yps.b + yo[0].b, ysum.b)
                    S.op("dve", lambda: nc.vector.tensor_tensor(out=ysum.t[:], in0=ysum.t[:], in1=yo[1].t[:], op=ALU.add), ysum.b + yo[1].b, ysum.b)
                    S.op("dve", lambda: nc.vector.tensor_tensor(out=ysum.t[:], in0=ysum.t[:], in1=X.zsk.t[:], op=ALU.mult), ysum.b + X.zsk.b, ysum.b)
                    S.op("act", lambda: nc.scalar.activation(out=yo[0].t[:], in_=ysum.t[:], func=AF.Square), ysum.b, yo[0].b)
                    S.op("dve", lambda: nc.vector.tensor_reduce(out=ssq.t[:, 0:1], in_=yo[0].t[:], axis=mybir.AxisListType.X, op=ALU.add),
                         yo[0].b, ssq.b)
                    S.op("act", lambda: nc.scalar.activation(out=ssq.t[:, 1:2], in_=ssq.t[:, 0:1], func=AF.Ln, bias=epsc.t[:], scale=1.0 / 512),
                         ssq.b + epsc.b, ssq.b)
                    S.op("act", lambda: nc.scalar.activation(out=ssq.t[:, 1:2], in_=ssq.t[:, 1:2], func=AF.Exp, scale=-0.5), ssq.b, ssq.b)
                    S.op("dve", lambda: nc.vector.scalar_tensor_tensor(out=dtok.t[:], in0=ysum.t[:], scalar=ssq.t[:, 1:2], in1=gssm.t[:],
                                                                       op0=ALU.mult, op1=ALU.mult), ysum.b + ssq.b + gssm.b, dtok.b)
                    for c4 in range(4):
                        S.op("pe", lambda: nc.tensor.transpose(out=ptb_ap[:, c4 * 128:(c4 + 1) * 128], in_=dtok.t[:, c4 * 128:(c4 + 1) * 128],
                                                               identity=identb.t[:]), dtok.b + identb.b, pbA[3].b, inc=(c4 == 3))
                    S.op("act", lambda: nc.scalar.copy(out=CD.t[:, 4:8, blk * 128:(blk + 1) * 128],
                                                       in_=ptb_ap[:, 0:512].rearrange("p (c l) -> p c l", c=4)), pbA[3].b, CD.b[4:8])

                ssd_front(0)
                for blk in range(4):
                    ssd_mid(blk)
                    if blk + 1 < 4:
                        ssd_loads(blk + 1)
                        ssd_front(blk + 1)
                    ssd_back(blk)
                S.tag = "fin"
                if dbg:
                    S.dma("pool", CDd[j][t], CD.t[:], CD.b, [DB("CDd", j, t)], "wcd")
                for oc in range(8):
                    yps = pbA[oc % 2]
                    for k in range(8):
                        mm(yps.t[:], wo1.t[:, k, oc * 128:(oc + 1) * 128], CD.t[:, k, :], k == 0, k == 7, wo1.b + [CD.b[k]], yps.b, k == 7)
                    S.op("dve", lambda: nc.vector.tensor_tensor(out=x1c.t[:, oc, :], in0=yps.t[:], in1=x1c.t[:, oc, :], op=ALU.add),
                         yps.b + [x1c.b[oc]], [x1c.b[oc]])
                for c in range(8):
                    s = sqf[c % 2]
                    S.op("act", lambda: nc.scalar.activation(out=s.t[:], in_=x1c.t[:, c, :], func=AF.Square), [x1c.b[c]], s.b)
                    mm(pbA[2].t[:], onesb.t[:], s.t[:], c == 0, c == 7, s.b + onesb.b, pbA[2].b, True)
                S.op("act", lambda: nc.scalar.activation(out=rstd.t[:], in_=pbA[2].t[:], func=AF.Ln, bias=epsc.t[:], scale=1.0 / 1024),
                     pbA[2].b + epsc.b, rstd.b)
                S.op("act", lambda: nc.scalar.activation(out=rstd.t[:], in_=rstd.t[:], func=AF.Exp, scale=-0.5), rstd.b, rstd.b)
                S.gbegin()
                for c in range(8):
                    S.op("dve", lambda: nc.vector.scalar_tensor_tensor(out=x1c.t[:, c, :], in0=x1c.t[:, c, :], scalar=colp.t[:, 16 + c:17 + c],
                                                                       in1=rstd.t[:], op0=ALU.mult, op1=ALU.mult),
                         [x1c.b[c]] + rstd.b + colp.b, [x1c.b[c]])
                    if c % 2 == 1:
                        S.dma("pool", y_out[j][t, :, c - 1:c + 1, :], x1c.t[:, c - 1:c + 1, :], x1c.b[c - 1:c + 1], [DB("yout", j, t)], "yw")
                S.gend()
        S.barrier()
    es.close()
    print("instr counts", S.check_deadlock(), "max sem", max(S.cnt.values()), "nsem", len(S.cnt))
    return nc


QPERM = [0, 4, 1, 5, 2, 6, 3, 7]


def _consts():
    cmat = np.zeros((128, 9, 128), np.float32)
    cmat[:, 0, :] = np.eye(128)
    cmat[:, 1, :] = 1.0
    R = np.zeros((128, 128), np.float32)
    for f2 in range(128):
        if f2 % 64 < 32:
            R[f2 + 32, f2] = -1.0
        else:
            R[f2 - 32, f2] = 1.0
    cmat[:, 2, :] = R
    lp = np.arange(128)[:, None]
    l = np.arange(128)[None, :]
    cmat[:, 3, :] = (lp <= l)
    cmat[:, 4, :] = (lp >= l)
    cmask = np.zeros((128, 4, 512), np.float32)
    kk = np.arange(128)[:, None]
    qq = np.arange(128)[None, :]
    cmask[:, 0, :] = np.tile((qq <= kk).astype(np.float32), (1, 4))
    cmask[:, 1, :] = np.tile((kk <= qq).astype(np.float32), (1, 4))
    s = kk
    cmask[:, 2, :] = np.tile(np.where(qq >= s, 0.0, -30000.0).astype(np.float32), (1, 4))
    cmask[:, 3, :] = np.tile(np.where(qq <= s, 0.0, -30000.0).astype(np.float32), (1, 4))
    return cmat, cmask


def _cols(v, nch):
    return np.ascontiguousarray(np.asarray(v, np.float32).reshape(nch, 128).T)


def prep_weights(p):
    d = {}
    w_in0 = np.asarray(p["w_in0"][0], np.float32)
    hp = np.concatenate([np.arange(h * 64, (h + 1) * 64) for h in QPERM])
    cols = np.arange(2816)
    cols[1536:2048] = 1536 + hp
    cols[2304:2816] = 2304 + hp
    d["w_in0"] = np.ascontiguousarray(w_in0[:, cols])
    w_out0 = np.asarray(p["w_out0"][0], np.float32)
    rows = np.arange(1024)
    rows[512:1024] = 512 + hp
    d["w_out0"] = np.ascontiguousarray(w_out0[rows, :])
    d["w_in1"] = np.ascontiguousarray(np.asarray(p["w_in1"][0], np.float32))
    d["w_out1"] = np.ascontiguousarray(np.asarray(p["w_out1"][0], np.float32))
    colp = np.zeros((128, 64), np.float32)
    colp[:, 0:8] = _cols(p["norm_g"][0], 8)
    colp[:, 8:16] = _cols(p["norm_g"][1], 8)
    colp[:, 16:24] = _cols(p["final_norm_g"], 8)
    colp[:, 24:28] = _cols(p["conv_b"][0], 4)
    colp[:, 28:32] = _cols(p["conv_ln_g"][0], 4)
    colp[:, 32:36] = _cols(p["conv_ln_b"][0], 4)
    sink = np.asarray(p["sink"][0], np.float32)
    colp[64:128, 36:40] = sink[0:4][None, :]
    colp[0:64, 36:40] = sink[4:8][None, :]
    colp[:, 40:48] = _cols(p["ssm_conv_b"][0], 8)
    colp[:, 48] = np.asarray(p["diff_norm_g"][0], np.float32)
    d["colp"] = colp
    d["cw"] = np.ascontiguousarray(np.asarray(p["conv_w"][0], np.float32).reshape(31, 4, 128).transpose(2, 1, 0))
    d["scw"] = np.ascontiguousarray(np.asarray(p["ssm_conv_w"][0], np.float32).reshape(5, 8, 128).transpose(2, 1, 0))
    rowp = np.zeros((1, 1024), np.float32)
    rowp[0, 0:64] = p["lambda_q1"][0]
    rowp[0, 64:128] = p["lambda_k1"][0]
    rowp[0, 128:192] = p["lambda_q2"][0]
    rowp[0, 192:256] = p["lambda_k2"][0]
    rowp[0, 256:264] = p["dt_bias_f"][0]
    rowp[0, 264:272] = p["dt_bias_b"][0]
    rowp[0, 272:280] = p["a_log_f"][0]
    rowp[0, 280:288] = p["a_log_b"][0]
    rowp[0, 288:296] = p["d_skip"][0]
    rowp[0, 296:424] = p["diff_norm_g"][0]
    rowp[0, 512:1024] = p["ssm_norm_g"][0]
    d["rowp"] = rowp
    d["cmat"], d["cmask"] = _consts()
    return d


def prep_job(x, q0, nq):
    Sx = x.shape[0]
    n = Sx // T
    xpad = np.zeros((Sx + 2 * H, 1024), np.float32)
    xpad[H:H + Sx] = x
    inv = (1.0 / (10000.0 ** (np.arange(0, 64, 2, dtype=np.float32) / 64))).astype(np.float32)
    xt = np.empty((n, 128, 8, TW), np.float32)
    cs = np.empty((n, 128, 2, TW), np.float32)
    kv = np.ones((n, 128, 2), np.float32)
    hf = np.ones((n, 128, 2), np.float32)
    kf = np.ones((128, 2, n * 4), np.float32)
    for i in range(n):
        tt = (q0 + i) % n
        seg = xpad[tt * T:tt * T + TW]
        xt[i] = seg.T.reshape(8, 128, TW).transpose(1, 0, 2)
        pos = (np.arange(TW, dtype=np.float32) + np.float32(tt * T - H))
        f = pos[None, :] * inv[:, None]
        emb = np.concatenate([f, f, f, f], axis=0)
        cs[i, :, 0, :] = np.cos(emb)
        cs[i, :, 1, :] = np.sin(emb)
        if tt == 0:
            kv[i, :, 0] = 0.0
            hf[i, :, 0] = 0.0
            kf[:, 0, i * 4] = 0.0
        if tt == n - 1:
            kv[i, :, 1] = 0.0
            hf[i, :, 1] = 0.0
            kf[:, 1, i * 4 + 3] = 0.0
    return xt, cs, kv, hf, kf


JOBS = [dict(n=8, nq=8), dict(n=8, nq=8), dict(n=16, nq=4)]
_NC_CACHE = {}


def kernel(x_prompt, x_sample, **p):
    x_prompt = np.asarray(x_prompt, np.float32)
    x_sample = np.asarray(x_sample, np.float32)
    wd = prep_weights(p)
    key = "main"
    if key not in _NC_CACHE:
        _NC_CACHE[key] = build(JOBS)
    nc = _NC_CACHE[key]
    in_maps = []
    for c in range(8):
        m = dict(wd)
        specs = [(x_sample[2 * c], 0, 8), (x_sample[2 * c + 1], 0, 8), (x_prompt[c // 4], (c % 4) * 4, 4)]
        for j, (xs, q0, nq) in enumerate(specs):
            xt, cs, kv, hf, kf = prep_job(xs, q0, nq)
            m[f"xt{j}"], m[f"cs{j}"], m[f"kv{j}"], m[f"hf{j}"], m[f"kf{j}"] = xt, cs, kv, hf, kf
        in_maps.append(m)
    res = run_bass_kernel_spmd(nc, in_maps, core_ids=list(range(8)))
    y_prompt = np.empty((2, 8192, 1024), np.float32)
    y_sample = np.empty((16, 4096, 1024), np.float32)

    def untile(yt):
        nq = yt.shape[0]
        return yt.transpose(0, 3, 2, 1).reshape(nq * T, 1024)

    for c in range(8):
        r = res.results[c]
        y_sample[2 * c] = untile(r["yt0"])
        y_sample[2 * c + 1] = untile(r["yt1"])
        q = c % 4
        y_prompt[c // 4, q * 2048:(q + 1) * 2048] = untile(r["yt2"])
    return (y_prompt, y_sample)
```
